# Optimizing a Trainium2 kernel written in Bass

```python
import math
import jax, jax.numpy as jnp
from jax import lax
import numpy as np

D_MODEL = 1024
BATCH = 32
SEQ = 2048
DEPTH = 2

CHUNK = 64
HEAD_DIM = 64
N_MIX_HEADS = D_MODEL // HEAD_DIM
ATT_HEADS = N_MIX_HEADS // 4
RWKV_HEADS = (N_MIX_HEADS - ATT_HEADS) // 2
RET_HEADS = N_MIX_HEADS - ATT_HEADS - RWKV_HEADS
RWKV_W = RWKV_HEADS * HEAD_DIM
RET_W = RET_HEADS * HEAD_DIM
ATT_W = ATT_HEADS * HEAD_DIM
D_MIX = RWKV_W + RET_W + ATT_W

DECAY_LORA = 64
AAA_LORA = 64
GATE_LORA = 128
RWKV_SPLITS = (RWKV_W, RWKV_W, RWKV_W, DECAY_LORA, AAA_LORA, GATE_LORA)
RWKV_COLS = sum(RWKV_SPLITS)
RET_COLS = 4 * RET_W
ATT_COLS = 3 * ATT_W
N_IN_COLS = RWKV_COLS + RET_COLS + ATT_COLS

BAND_PREV_CHUNKS = 8
BAND = (BAND_PREV_CHUNKS + 1) * CHUNK
REL_CLIP = 128
REL_TABLE = (CHUNK - 1) + REL_CLIP + 1

D_FF = 256 * ((8 * D_MODEL // 3 + 255) // 256)
CONV_W = 3

ALPHA = (2 * DEPTH) ** 0.25
BETA = (8 * DEPTH) ** -0.25
ROPE_BASE = 10000.0
LN_EPS = 1e-5
RWKV_GN_EPS = 64e-5
RET_GN_EPS = 1e-5

kernel_name = "hybrid_rwkv7_retnet_chunkattn_deepnorm"


def _offsets(sizes):
    out, acc = [], 0
    for s in sizes[:-1]:
        acc += s
        out.append(acc)
    return out


def layer_norm(x, g, b, eps=LN_EPS):
    xf = x.astype(jnp.float32)
    mu = jnp.mean(xf, axis=-1, keepdims=True)
    var = jnp.mean(jnp.square(xf - mu), axis=-1, keepdims=True)
    y = (xf - mu) * lax.rsqrt(var + eps) * g.astype(jnp.float32) + b.astype(jnp.float32)
    return y.astype(x.dtype)


def head_norm(y, g, b, eps):
    H, d = y.shape[-2], y.shape[-1]
    mu = jnp.mean(y, axis=-1, keepdims=True)
    var = jnp.mean(jnp.square(y - mu), axis=-1, keepdims=True)
    yn = (y - mu) * lax.rsqrt(var + eps)
    return yn * g.astype(jnp.float32).reshape(H, d) + b.astype(jnp.float32).reshape(H, d)


def token_shift(z, mu):
    prev = jnp.pad(z, ((0, 0), (1, 0), (0, 0)))[:, :-1]
    return z + (prev - z) * mu


def rope(t, pos):
    half = t.shape[-1] // 2
    inv = ROPE_BASE ** (-jnp.arange(half, dtype=jnp.float32) / half)
    ang = pos[:, None] * inv[None, :]
    cos = jnp.cos(ang)[:, None, :]
    sin = jnp.sin(ang)[:, None, :]
    t1, t2 = t[..., :half], t[..., half:]
    return jnp.concatenate([t1 * cos - t2 * sin, t1 * sin + t2 * cos], axis=-1)


def rwkv7_time_mix(z, mu, w0, w_up, a0, a_up, g_up, k_k, k_a, r_k, ln_g, ln_b):
    B, S, _ = z.shape
    H, N = RWKV_HEADS, HEAD_DIM
    f32 = jnp.float32
    z = token_shift(z.astype(f32), mu.astype(f32))
    r, k, v, wl, al, gl = jnp.split(z, _offsets(RWKV_SPLITS), axis=-1)
    w_raw = -jax.nn.softplus(-(w0.astype(f32) + jnp.tanh(wl) @ w_up.astype(f32))) - 0.5
    decay = jnp.exp(-jnp.exp(w_raw))
    a = jax.nn.sigmoid(a0.astype(f32) + al @ a_up.astype(f32))
    g = jax.nn.sigmoid(gl) @ g_up.astype(f32)

    def heads(t):
        return t.reshape(B, S, H, N)

    kk = heads(k * k_k.astype(f32))
    kk = kk / jnp.maximum(jnp.sqrt(jnp.sum(kk * kk, axis=-1, keepdims=True)), 1e-12)
    k = heads(k * (1.0 + (a - 1.0) * k_a.astype(f32)))
    r, v, decay, a = heads(r), heads(v), heads(decay), heads(a)
    a_vec = -kk
    b_vec = kk * a

    def step(state, inp):
        r_t, w_t, k_t, v_t, a_t, b_t = inp
        sa = jnp.einsum('bhvk,bhk->bhv', state, a_t)
        state = (state * w_t[:, :, None, :] + sa[..., None] * b_t[:, :, None, :]
                 + v_t[..., None] * k_t[:, :, None, :])
        return state, jnp.einsum('bhvk,bhk->bhv', state, r_t)

    xs = tuple(jnp.moveaxis(t, 1, 0) for t in (r, decay, k, v, a_vec, b_vec))
    _, y = lax.scan(step, jnp.zeros((B, H, N, N), f32), xs)
    y = jnp.moveaxis(y, 0, 1)
    bonus = jnp.sum(r * k * r_k.astype(f32), axis=-1, keepdims=True) * v
    y = head_norm(y, ln_g, ln_b, RWKV_GN_EPS) + bonus
    return y.reshape(B, S, RWKV_W) * g


def retention_mix(z, gn_g, gn_b):
    B, S, _ = z.shape
    H, d, C = RET_HEADS, HEAD_DIM, CHUNK
    nc = S // C
    f32 = jnp.float32
    q, k, v, g = jnp.split(z.astype(f32), [RET_W, 2 * RET_W, 3 * RET_W], axis=-1)
    pos = jnp.arange(S, dtype=f32)
    q = rope(q.reshape(B, S, H, d), pos)
    k = rope(k.reshape(B, S, H, d), pos) * (d ** -0.5)
    v = v.reshape(B, S, H, d)
    log_g = jnp.log(1.0 - jnp.exp2(-5.0 - jnp.arange(H, dtype=f32)))
    q = q.reshape(B, nc, C, H, d)
    k = k.reshape(B, nc, C, H, d)
    v = v.reshape(B, nc, C, H, d)
    idx = jnp.arange(C, dtype=f32)
    dmat = jnp.exp(log_g[:, None, None] * jnp.abs(idx[:, None] - idx[None, :]))
    scores = jnp.einsum('bnihd,bnjhd->bnhij', q, k) * dmat
    intra = jnp.einsum('bnhij,bnjhe->bnihe', scores, v)
    k_dec = jnp.exp(log_g[None, :] * (C - 1.0 - idx)[:, None])
    u = jnp.einsum('bnjhd,bnjhe->nbhde', k * k_dec[None, None, :, :, None], v)
    chunk_decay = jnp.exp(log_g * C)[None, :, None, None]

    def step(state, u_n):
        return state * chunk_decay + u_n, state

    _, r_prev = lax.scan(step, jnp.zeros((B, H, d, d), f32), u)
    q_dec = jnp.exp(log_g[None, :] * (idx + 1.0)[:, None])
    inter = jnp.einsum('bnihd,nbhde->bnihe', q, r_prev) * q_dec[None, None, :, :, None]
    y = (intra + inter).reshape(B, S, H, d)
    y = head_norm(y, gn_g, gn_b, RET_GN_EPS).reshape(B, S, RET_W)
    return y * jax.nn.silu(g)


def chunk_band_attention(z, rel_bias):
    B, S, _ = z.shape
    H, d, C = ATT_HEADS, HEAD_DIM, CHUNK
    nc = S // C
    pad = BAND_PREV_CHUNKS * C
    f32 = jnp.float32
    q, k, v = jnp.split(z.astype(f32), [ATT_W, 2 * ATT_W], axis=-1)
    q = q.reshape(B, S, H, d) * (d ** -0.5)
    k_pad = jnp.pad(k.reshape(B, S, H, d), ((0, 0), (pad, 0), (0, 0), (0, 0)))
    v_pad = jnp.pad(v.reshape(B, S, H, d), ((0, 0), (pad, 0), (0, 0), (0, 0)))
    i = jnp.arange(C)
    j = jnp.arange(BAND)
    rel = (i[:, None] + pad) - j[None, :]
    rel_idx = jnp.clip(rel, -(C - 1), REL_CLIP) + (C - 1)
    bias = rel_bias.astype(f32)[:, rel_idx]

    def one_chunk(n):
        start = n * C
        q_n = lax.dynamic_slice_in_dim(q, start, C, axis=1)
        k_n = lax.dynamic_slice_in_dim(k_pad, start, BAND, axis=1)
        v_n = lax.dynamic_slice_in_dim(v_pad, start, BAND, axis=1)
        s = jnp.einsum('bihd,bjhd->bhij', q_n, k_n) + bias[None]
        valid = (start + j) >= pad
        s = jnp.where(valid[None, None, None, :], s, -1e30)
        p = jax.nn.softmax(s, axis=-1)
        return jnp.einsum('bhij,bjhd->bihd', p, v_n)

    out = lax.map(one_chunk, jnp.arange(nc))
    return jnp.moveaxis(out, 0, 1).reshape(B, S, ATT_W)


def conv_gated_ffn(x, w_up, conv_w, conv_b, w_down):
    S = x.shape[1]
    u = x @ w_up
    u_pad = jnp.pad(u, ((0, 0), (CONV_W - 1, 0), (0, 0)))
    c = conv_b
    for t in range(CONV_W):
        c = c + conv_w[t] * u_pad[:, t:t + S]
    gate, val = jnp.split(c, 2, axis=-1)
    return (jax.nn.silu(gate) * val) @ w_down


def setup_inputs(seed: int = 0) -> dict:
    key = jax.random.key(seed)
    ks = jax.random.split(key, 32)
    f32 = jnp.float32
    nrm = lambda k, s: jax.random.normal(k, s, f32)
    L = DEPTH
    return {
        "x": nrm(ks[0], (BATCH, SEQ, D_MODEL)),
        "ln_in_g": 1.0 + 0.02 * nrm(ks[1], (D_MODEL,)),
        "ln_in_b": 0.02 * nrm(ks[2], (D_MODEL,)),
        "w_in": nrm(ks[3], (L, D_MODEL, N_IN_COLS)) * D_MODEL ** -0.5,
        "rw_mu": jax.random.uniform(ks[4], (L, RWKV_COLS), f32),
        "rw_w0": jax.random.uniform(ks[5], (L, RWKV_W), f32, minval=-6.0, maxval=-1.0),
        "rw_w_up": nrm(ks[6], (L, DECAY_LORA, RWKV_W)) * 0.5 * DECAY_LORA ** -0.5,
        "rw_a0": 0.1 * nrm(ks[7], (L, RWKV_W)),
        "rw_a_up": nrm(ks[8], (L, AAA_LORA, RWKV_W)) * AAA_LORA ** -0.5,
        "rw_g_up": nrm(ks[9], (L, GATE_LORA, RWKV_W)) * GATE_LORA ** -0.5,
        "rw_k_k": 0.85 + 0.05 * nrm(ks[10], (L, RWKV_W)),
        "rw_k_a": 1.0 + 0.05 * nrm(ks[11], (L, RWKV_W)),
        "rw_r_k": 0.1 * nrm(ks[12], (L, RWKV_HEADS, HEAD_DIM)),
        "rw_ln_g": 1.0 + 0.02 * nrm(ks[13], (L, RWKV_W)),
        "rw_ln_b": 0.02 * nrm(ks[14], (L, RWKV_W)),
        "ret_gn_g": 1.0 + 0.02 * nrm(ks[15], (L, RET_W)),
        "ret_gn_b": 0.02 * nrm(ks[16], (L, RET_W)),
        "attn_rel_bias": 0.1 * nrm(ks[17], (L, ATT_HEADS, REL_TABLE)),
        "w_out": nrm(ks[18], (L, D_MIX, D_MODEL)) * D_MIX ** -0.5 * BETA,
        "ln1_g": 1.0 + 0.02 * nrm(ks[19], (L, D_MODEL)),
        "ln1_b": 0.02 * nrm(ks[20], (L, D_MODEL)),
        "ffn_w_up": nrm(ks[21], (L, D_MODEL, 2 * D_FF)) * D_MODEL ** -0.5,
        "ffn_conv_w": nrm(ks[22], (L, CONV_W, 2 * D_FF)) * CONV_W ** -0.5,
        "ffn_conv_b": 0.02 * nrm(ks[23], (L, 2 * D_FF)),
        "ffn_w_down": nrm(ks[24], (L, D_FF, D_MODEL)) * D_FF ** -0.5 * BETA,
        "ln2_g": 1.0 + 0.02 * nrm(ks[25], (L, D_MODEL)),
        "ln2_b": 0.02 * nrm(ks[26], (L, D_MODEL)),
    }


def reference(x, ln_in_g, ln_in_b, w_in, rw_mu, rw_w0, rw_w_up, rw_a0, rw_a_up, rw_g_up,
              rw_k_k, rw_k_a, rw_r_k, rw_ln_g, rw_ln_b, ret_gn_g, ret_gn_b, attn_rel_bias,
              w_out, ln1_g, ln1_b, ffn_w_up, ffn_conv_w, ffn_conv_b, ffn_w_down, ln2_g, ln2_b):
    x = layer_norm(x, ln_in_g, ln_in_b)
    for l in range(DEPTH):
        z = x @ w_in[l]
        z_rwkv = z[..., :RWKV_COLS]
        z_ret = z[..., RWKV_COLS:RWKV_COLS + RET_COLS]
        z_att = z[..., RWKV_COLS + RET_COLS:]
        y_rwkv = rwkv7_time_mix(z_rwkv, rw_mu[l], rw_w0[l], rw_w_up[l], rw_a0[l], rw_a_up[l],
                                rw_g_up[l], rw_k_k[l], rw_k_a[l], rw_r_k[l], rw_ln_g[l], rw_ln_b[l])
        y_ret = retention_mix(z_ret, ret_gn_g[l], ret_gn_b[l])
        y_att = chunk_band_attention(z_att, attn_rel_bias[l])
        y = jnp.concatenate([y_rwkv, y_ret, y_att], axis=-1).astype(x.dtype)
        x = layer_norm(ALPHA * x + y @ w_out[l], ln1_g[l], ln1_b[l])
        f = conv_gated_ffn(x, ffn_w_up[l], ffn_conv_w[l], ffn_conv_b[l], ffn_w_down[l])
        x = layer_norm(ALPHA * x + f.astype(x.dtype), ln2_g[l], ln2_b[l])
    return x
```

```python
import numpy as np
from contextlib import ExitStack
import concourse.bass as bass
import concourse.mybir as mybir
from concourse.bass_utils import run_bass_kernel_spmd

F32 = mybir.dt.float32
BF16 = mybir.dt.bfloat16
AF = mybir.ActivationFunctionType
ALU = mybir.AluOpType

EPOCH = 30000
NDMASEM = 6

D = 1024
TB = 512
NT = 4
L = 2
DFF = 2816
NJ = 22
OFF_RET = 1408
OFF_ATT = 2944
C0 = float(np.exp(-0.5))
ALPHA = float(4.0 ** 0.25)
LN_EPS = 1e-5
NPP = 44
NSLOT = 98
STAGES = {"att", "ret", "rwkv", "out", "ffn"}
CUT = None


class MK:
    ENGS = ("pe", "act", "dve", "pool", "sp")

    def __init__(self, nc):
        self.nc = nc
        self.es = ExitStack()
        self.ops = {e: [] for e in self.ENGS}
        self.count = {}
        self.sems = {}
        self.known = {e: {} for e in self.ENGS}
        self.lastw = {}
        self.readers = {}
        self.dma_rr = {e: 0 for e in self.ENGS}
        self.muted = False
        for e in self.ENGS:
            self.count[e] = 0
            self.sems[e] = []
        self.dmaprod = {}
        for q in ("sp", "pool", "act"):
            for j in range(NDMASEM):
                p = f"dma_{q}_{j}"
                self.count[p] = 0
                self.sems[p] = []
                self.dmaprod[(q, j)] = p

    def sb(self, name, shape, dtype):
        return self.es.enter_context(self.nc.sbuf_tensor("sb_" + name, list(shape), dtype))

    def ps(self, name, shape, dtype=F32):
        return self.es.enter_context(self.nc.psum_tensor("ps_" + name, list(shape), dtype))

    def _sem(self, prod, tick):
        if prod.startswith("dma_"):
            per = EPOCH // 16
            mult = 16
        else:
            per = EPOCH
            mult = 1
        ep = (tick - 1) // per
        lst = self.sems[prod]
        while len(lst) <= ep:
            lst.append(self.es.enter_context(self.nc.semaphore(f"s_{prod}_{len(lst)}")))
        return lst[ep], ((tick - 1) % per + 1) * mult

    def _deps(self, eng, reads, writes):
        deps = {}
        lastw = self.lastw
        for k in reads:
            pt = lastw.get(k)
            if pt is not None and pt[1] > deps.get(pt[0], 0):
                deps[pt[0]] = pt[1]
        for k in writes:
            pt = lastw.get(k)
            if pt is not None and pt[1] > deps.get(pt[0], 0):
                deps[pt[0]] = pt[1]
            rd = self.readers.get(k)
            if rd:
                for p, t in rd.items():
                    if t > deps.get(p, 0):
                        deps[p] = t
        waits = []
        kn = self.known[eng]
        for p, t in deps.items():
            if p == eng and eng == "pe":
                continue
            if kn.get(p, 0) >= t:
                continue
            kn[p] = t
            waits.append(self._sem(p, t))
        return waits

    def _commit(self, prod, tick, reads, writes):
        for k in reads:
            self.readers.setdefault(k, {})[prod] = tick
        for k in writes:
            self.lastw[k] = (prod, tick)
            self.readers[k] = {}

    def op(self, eng, fn, reads=(), writes=()):
        if self.muted:
            return
        waits = self._deps(eng, reads, writes)
        self.count[eng] += 1
        tick = self.count[eng]
        sem, _ = self._sem(eng, tick)
        self.ops[eng].append((waits, fn, sem, 1))
        self._commit(eng, tick, reads, writes)

    def dma(self, q, out, in_, reads=(), writes=()):
        if self.muted:
            return
        j = self.dma_rr[q]
        self.dma_rr[q] = (j + 1) % NDMASEM
        prod = self.dmaprod[(q, j)]
        waits = self._deps(q, reads, writes)
        prev = self.count[prod]
        if prev > 0 and self.known[q].get(prod, 0) < prev:
            self.known[q][prod] = prev
            waits.append(self._sem(prod, prev))
        self.count[prod] += 1
        tick = self.count[prod]
        sem, _ = self._sem(prod, tick)
        fn = lambda e, out=out, in_=in_: e.dma_start(out=out, in_=in_)
        self.ops[q].append((waits, fn, sem, 16))
        self._commit(prod, tick, reads, writes)

    def final_wait(self, eng, keys):
        waits = self._deps(eng, keys, ())
        self.ops[eng].append((waits, None, None, 0))

    def emit(self):
        block = self.es.enter_context(self.nc.Block())
        ops = self.ops

        def replay(e, lst):
            for waits, fn, sem, inc in lst:
                for (s, v) in waits:
                    e.wait_ge(s, v)
                if fn is not None:
                    fn(e).then_inc(sem, inc)

        @block.tensor
        def _(e):
            replay(e, ops["pe"])

        @block.scalar
        def _(e):
            replay(e, ops["act"])

        @block.vector
        def _(e):
            replay(e, ops["dve"])

        @block.gpsimd
        def _(e):
            replay(e, ops["pool"])

        @block.sync
        def _(e):
            replay(e, ops["sp"])

    def close(self):
        self.es.close()


class Buf:
    def __init__(self, ap, keys):
        self.ap = ap
        self.keys = keys

    def v(self, pat, **kw):
        return self.ap.rearrange(pat, **kw)


def build(nseq, nblk, taps=None):
    nc = bass.Bass("TRN2", target_bir_lowering=False)
    mk = MK(nc)
    S = nblk * TB

    def din(name, shape):
        return nc.dram_tensor(name, list(shape), F32, kind="ExternalInput").ap()

    x_d = din("x", [nseq, S, D])
    w_in_d = din("w_in", [L, D, 3712])
    w_out_d = din("w_out", [L, D, D])
    w_up_d = din("w_up", [L, D, 2 * DFF])
    w_dn_d = din("w_down", [L, DFF, D])
    lng_d = din("lng", [1 + 2 * L, D])
    lnb_d = din("lnb", [1 + 2 * L, D])
    pp_d = din("pp", [128, L, NPP])
    cw_d = din("convw", [128, L, 44 * 3])
    cb_d = din("convb", [128, L, 44])
    lorab_d = din("lorab", [128, L, 384])
    gup_d = din("gup", [128, L, 384])
    bias_d = din("biasT", [L, 128, 4 * 5 * 128])
    ident_d = din("ident", [128, 128])
    bones_d = din("bones", [128, 128])
    prot_d = din("prot", [128, 128])
    cos_d = din("cosT", [128, 2048])
    sin_d = din("sinT", [128, 2048])
    dmat_d = din("dmatT", [128, 768])
    kdq_d = din("kdq", [128, 384])
    cd_d = din("cdt", [128, 3])
    msk1_d = din("msk4", [128, 512])
    mskL_d = din("mskL", [128, 128])
    cmask_d = din("cmask", [128, 512])
    out_d = nc.dram_tensor("out", [nseq, S, D], F32, kind="ExternalOutput").ap()
    tap_d = {}
    if taps:
        for nm, shp in taps.items():
            tap_d[nm] = nc.dram_tensor("tap_" + nm, list(shp), F32, kind="ExternalOutput").ap()

    ident32 = mk.sb("ident32", [128, 128], F32)
    identbf = mk.sb("identbf", [128, 128], BF16)
    bonesbf = mk.sb("bonesbf", [128, 128], BF16)
    bones32 = mk.sb("bones32", [128, 128], F32)
    prot32 = mk.sb("prot32", [128, 128], F32)
    dmatT = mk.sb("dmatT", [128, 768], F32)
    kdq = mk.sb("kdq", [128, 2, 3, 64], F32)
    cdt = mk.sb("cdt", [128, 3, 1], F32)
    msk4 = mk.sb("msk4", [128, 512], F32)
    mskL = mk.sb("mskL", [128, 1, 128], F32)
    cmask = mk.sb("cmask", [128, 512], F32)
    pp = mk.sb("pp", [128, L, NPP], F32)
    cw = mk.sb("cw", [128, L, 44, 3], F32)
    cb = mk.sb("cb", [128, L, 44], F32)
    lorab = mk.sb("lorab", [128, L, 384], BF16)
    gup = mk.sb("gup", [128, L, 384], BF16)
    epst = mk.sb("epst", [128, 4], F32)
    xres = mk.sb("xres", [128, NT, D], F32)
    xT = mk.sb("xT", [128, 8, TB], BF16)
    yT = mk.sb("yT", [128, 8, TB], BF16)
    s32 = [mk.sb(f"s32_{i}", [128, D], F32) for i in range(2)]
    lng_t = mk.sb("lng_t", [128, D], F32)
    lnb_t = mk.sb("lnb_t", [128, D], F32)
    W = [mk.sb(f"W{i}", [128, 8, 512], BF16) for i in range(3)]
    kTa = [mk.sb(f"kTa{l}", [128, 2, 1024], BF16) for l in range(L)]
    vA = [mk.sb(f"vA{l}", [128, 8, 4, 65], BF16) for l in range(L)]
    R32 = [mk.sb(f"R32_{l}", [128, 3, 64], F32) for l in range(L)]
    S32 = [mk.sb(f"S32_{l}", [128, 3, 64], F32) for l in range(L)]
    hal = [mk.sb(f"hal{l}", [128, 11], F32) for l in range(L)]
    chal = [mk.sb(f"chal{l}", [128, 44, 2], F32) for l in range(L)]
    lnst = [(mk.sb(f"st{i}", [128, 2, 6], F32), mk.sb(f"mv{i}", [128, 2], F32), mk.sb(f"sd{i}", [128, 1], F32),
             mk.sb(f"rstd{i}", [128, 1], F32), mk.sb(f"nmr{i}", [128, 1], F32)) for i in range(2)]
    pcs = mk.sb("pcs", [128, 3, 8], F32)
    rc = mk.sb("rc", [128, 4, 1], F32)
    A = mk.sb("A", [128, NSLOT, 512], BF16)
    B = [mk.ps(f"B{i}", [128, 512], F32) for i in range(8)]
    Bk = [f"B{i}" for i in range(8)]

    class Arena:
        def __init__(self):
            self.top = 0

        def reset(self):
            self.top = 0

        def bf(self, n):
            s0 = self.top
            self.top += n
            assert self.top <= NSLOT, self.top
            ap = A[:, s0:s0 + n, :].rearrange("p a b -> p (a b)")
            return Buf(ap, [("A", s) for s in range(s0, s0 + n)])

        def f32(self, n):
            b = self.bf(2 * n)
            return Buf(b.ap.bitcast(F32), b.keys)

    ar = Arena()

    def mm(out, lhsT, rhs, start, stop, r, w):
        mk.op("pe", lambda e: e.matmul(out, lhsT=lhsT, rhs=rhs, start=start, stop=stop), reads=r, writes=w)

    def tr(out, in_, idn, r, w):
        mk.op("pe", lambda e: e.transpose(out, in_, idn), reads=r, writes=w)

    def act(out, in_, func, r, w, bias=None, scale=1.0):
        if bias is None:
            mk.op("act", lambda e: e.activation(out=out, in_=in_, func=func, scale=scale), reads=r, writes=w)
        else:
            mk.op("act", lambda e: e.activation(out=out, in_=in_, func=func, bias=bias, scale=scale), reads=r, writes=w)

    def tt(eng, out, in0, in1, op, r, w):
        mk.op(eng, lambda e: e.tensor_tensor(out=out, in0=in0, in1=in1, op=op), reads=r, writes=w)

    def ts(eng, out, in0, s1, s2, op0, op1, r, w):
        if op1 is None:
            mk.op(eng, lambda e: e.tensor_scalar(out=out, in0=in0, scalar1=s1, scalar2=None, op0=op0), reads=r, writes=w)
        else:
            mk.op(eng, lambda e: e.tensor_scalar(out=out, in0=in0, scalar1=s1, scalar2=s2, op0=op0, op1=op1), reads=r, writes=w)

    def stt(out, in0, scalar, in1, op0, op1, r, w):
        mk.op("dve", lambda e: e.scalar_tensor_tensor(out=out, in0=in0, scalar=scalar, in1=in1, op0=op0, op1=op1),
              reads=r, writes=w)

    def cp(eng, out, in_, r, w):
        if eng == "act":
            mk.op("act", lambda e: e.activation(out=out, in_=in_, func=AF.Copy), reads=r, writes=w)
        else:
            mk.op(eng, lambda e: e.tensor_copy(out=out, in_=in_), reads=r, writes=w)

    def memset(eng, ap, val, w):
        mk.op(eng, lambda e: e.memset(ap, val), writes=w)

    def tap(name, ap, keys):
        if name in tap_d:
            n = ap.shape[1]
            for c0 in range(0, n, 512):
                mk.dma("pool", tap_d[name][:, c0:c0 + 512], ap[:, c0:c0 + 512], reads=keys, writes=["tap_" + name])

    def run_interleaved(gens):
        gens = list(gens)
        while gens:
            for g_ in list(gens):
                try:
                    next(g_)
                except StopIteration:
                    gens.remove(g_)

    wrr = [0]

    def wload(src, ncols, nk=8):
        i = wrr[0] % 3
        wrr[0] += 1
        dst = W[i][:, 0:nk, 0:ncols]
        mk.dma("pool", dst, src.rearrange("(kc p) n -> p kc n", p=128), writes=[f"W{i}"])
        return W[i], f"W{i}"

    mk.dma("sp", ident32[:], ident_d, writes=["ident32"])
    mk.dma("pool", identbf[:], ident_d, writes=["identbf"])
    mk.dma("pool", bonesbf[:], bones_d, writes=["bonesbf"])
    mk.dma("sp", bones32[:], bones_d, writes=["bones32"])
    mk.dma("sp", prot32[:], prot_d, writes=["prot32"])
    mk.dma("sp", dmatT[:], dmat_d, writes=["dmatT"])
    mk.dma("sp", kdq[:].rearrange("p a b c -> p (a b c)"), kdq_d, writes=["kdq"])
    mk.dma("sp", cdt[:].rearrange("p a b -> p (a b)"), cd_d, writes=["cdt"])
    mk.dma("sp", msk4[:], msk1_d, writes=["msk4"])
    mk.dma("sp", mskL[:].rearrange("p a b -> p (a b)"), mskL_d, writes=["mskL"])
    mk.dma("sp", cmask[:], cmask_d, writes=["cmask"])
    mk.dma("sp", pp[:], pp_d, writes=["pp"])
    mk.dma("sp", cw[:].rearrange("p l a b -> p l (a b)"), cw_d, writes=["cw"])
    mk.dma("sp", cb[:], cb_d, writes=["cb"])
    mk.dma("pool", lorab[:], lorab_d, writes=["lorab"])
    mk.dma("pool", gup[:], gup_d, writes=["gup"])
    ts("dve", bones32[:], bones32[:], 1.0 / 64.0, None, ALU.mult, None, ["bones32"], ["bones32"])
    memset("dve", epst[:, 0:1], LN_EPS, ["epst"])
    memset("dve", epst[:, 1:2], 64e-5, ["epst"])
    memset("dve", epst[:, 2:3], 1e-5, ["epst"])
    for l in range(L):
        pass
    omm = mk.sb("omm", [128, L, 11], F32)
    omka = mk.sb("omka", [128, L, 3], F32)
    ts("dve", omm[:], pp[:, :, 0:11], -1.0, 1.0, ALU.mult, ALU.add, ["pp"], ["omm"])
    ts("dve", omka[:], pp[:, :, 20:23], -1.0, 1.0, ALU.mult, ALU.add, ["pp"], ["omka"])
    for l in range(L):
        memset("dve", vA[l][:], 1.0, [("vA", l, i) for i in range(8)])

    PPk = ["pp", "omm", "omka"]

    def ln_a(t_, s_ap, s_key):
        i = t_ % 2
        st_, mv_, sd_, rs_, nm_ = lnst[i]
        k_ = f"lnst{i}"
        mk.op("dve", lambda e: e.bn_stats(out=st_[:, 0, :], in_=s_ap[:, 0:512]), reads=[s_key], writes=[k_])
        mk.op("dve", lambda e: e.bn_stats(out=st_[:, 1, :], in_=s_ap[:, 512:1024]), reads=[s_key], writes=[k_])
        mk.op("dve", lambda e: e.bn_aggr(out=mv_[:], in_=st_[:].rearrange("p a b -> p (a b)")), reads=[k_], writes=[k_])
        act(sd_[:], mv_[:, 1:2], AF.Sqrt, [k_, "epst"], [k_], bias=epst[:, 0:1])
        mk.op("dve", lambda e: e.reciprocal(out=rs_[:], in_=sd_[:]), reads=[k_], writes=[k_])
        stt(nm_[:], mv_[:, 0:1], -1.0, rs_[:], ALU.mult, ALU.mult, [k_], [k_])
        xk = ("xres", t_)
        act(xres[:, t_, :], s_ap, AF.Identity, [s_key, k_], [xk], bias=nm_[:], scale=rs_[:])
        tt("dve", xres[:, t_, :], xres[:, t_, :], lng_t[:], ALU.mult, [xk, "lng_t"], [xk])
        tt("dve", xres[:, t_, :], xres[:, t_, :], lnb_t[:], ALU.add, [xk, "lnb_t"], [xk])

    def ln_t(t_, pbank):
        xk = ("xres", t_)
        for hb in range(2):
            bk = pbank + hb
            for q in range(4):
                kc = hb * 4 + q
                tr(B[bk][:, q * 128:(q + 1) * 128], xres[:, t_, kc * 128:(kc + 1) * 128], ident32[:],
                   [xk, "ident32"], [Bk[bk]])
            cp("act" if hb == 0 else "dve", xT[:, hb * 4:hb * 4 + 4, t_ * 128:(t_ + 1) * 128],
               B[bk][:].rearrange("p (a b) -> p a b", b=128), [Bk[bk]], [("xT", t_)])

    def ln_tile(t_, s_ap, s_key, pbank):
        ln_a(t_, s_ap, s_key)
        ln_t(t_, pbank)

    XTK = [("xT", t_) for t_ in range(NT)]
    YTK = [("yT", c) for c in range(8)]

    def load_ln(idx):
        mk.dma("sp", lng_t[:], lng_d[idx].partition_broadcast(128), writes=["lng_t"])
        mk.dma("sp", lnb_t[:], lnb_d[idx].partition_broadcast(128), writes=["lnb_t"])

    def headnorm3(ys, epscol, gcol, bcol, scr, l):
        R3 = range(3)
        for i in R3:
            mm(B[i][:], bones32[:], ys[i].ap, True, True, ys[i].keys + ["bones32"], [Bk[i]])
        for i in R3:
            tt("dve", ys[i].ap, ys[i].ap, B[i][:], ALU.subtract, ys[i].keys + [Bk[i]], ys[i].keys)
        for i in R3:
            act(scr[i].ap, ys[i].ap, AF.Square, ys[i].keys, scr[i].keys)
        for i in R3:
            mm(B[3 + i][:], bones32[:], scr[i].ap, True, True, scr[i].keys + ["bones32"], [Bk[3 + i]])
        for i in R3:
            act(scr[i].ap, B[3 + i][:], AF.Sqrt, [Bk[3 + i], "epst"], scr[i].keys, bias=epst[:, epscol:epscol + 1])
        for i in R3:
            mk.op("dve", lambda e, sc=scr[i]: e.reciprocal(out=sc.ap, in_=sc.ap), reads=scr[i].keys, writes=scr[i].keys)
        for i in R3:
            tt("dve", ys[i].ap, ys[i].ap, scr[i].ap, ALU.mult, ys[i].keys + scr[i].keys, ys[i].keys)
        for i in R3:
            act(ys[i].ap, ys[i].ap, AF.Identity, ys[i].keys + PPk, ys[i].keys, bias=pp[:, l, bcol + i:bcol + i + 1],
                scale=pp[:, l, gcol + i:gcol + i + 1])

    for s in range(nseq):
        for l in range(L):
            memset("dve", R32[l][:], 0.0, [f"R32_{l}"])
            memset("dve", S32[l][:], 0.0, [f"S32_{l}"])
            memset("dve", hal[l][:], 0.0, [f"hal{l}"])
            memset("dve", chal[l][:], 0.0, [f"chal{l}"])
        for blk in range(nblk):
            load_ln(0)
            xin = x_d[s, blk * TB:(blk + 1) * TB, :].rearrange("(t p) d -> p t d", p=128)
            for t_ in range(NT):
                sb_ = s32[t_ % 2]
                sk = f"s32_{t_ % 2}"
                mk.dma("sp", sb_[:], xin[:, t_, :], writes=[sk])
                ln_a(t_, sb_[:], sk)
                if t_ >= 1:
                    ln_t(t_ - 1, 4 + 2 * ((t_ - 1) % 2))
            ln_t(NT - 1, 4 + 2 * ((NT - 1) % 2))
            for l in range(L):
                P = lambda c0, c1=None: pp[:, l, c0:(c0 + 1 if c1 is None else c1)]
                if "att" in STAGES:
                  ar.reset()
                  biasb = ar.bf(5)
                  qTa = ar.bf(2)
                  pTs = [ar.bf(1) for _ in range(3)]
                  yatt = ar.bf(1)
                  for hh_ in range(2):
                      mk.dma("pool", biasb.ap[:, hh_ * 1280:(hh_ + 1) * 1280], bias_d[l][:, hh_ * 1280:(hh_ + 1) * 1280],
                             writes=biasb.keys)
                  Wqk, Wqk_k = wload(w_in_d[l][:, OFF_ATT:OFF_ATT + 512], 512)
                  Wv_, Wv_k = wload(w_in_d[l][:, OFF_ATT + 512:OFF_ATT + 768], 256)
                  slot = blk % 2
                  for hp in range(2):
                      for kc in range(8):
                          mm(B[hp][:], Wqk[:, kc, hp * 128:(hp + 1) * 128], xT[:, kc, :], kc == 0, kc == 7,
                             [Wqk_k] + XTK, [Bk[hp]])
                      act(qTa.ap[:, hp * 512:(hp + 1) * 512], B[hp][:], AF.Copy, [Bk[hp]], qTa.keys, scale=0.125)
                  for hp in range(2):
                      for kc in range(8):
                          mm(B[2 + hp][:], Wqk[:, kc, 256 + hp * 128:256 + (hp + 1) * 128], xT[:, kc, :], kc == 0, kc == 7,
                             [Wqk_k] + XTK, [Bk[2 + hp]])
                      cp("dve", kTa[l][:, hp, slot * 512:(slot + 1) * 512], B[2 + hp][:], [Bk[2 + hp]],
                         [("kTa", l, slot * 4 + i) for i in range(4)])
                  for t_ in range(NT):
                      tg = (blk * 4 + t_) % 8
                      bk = 2 + t_ % 2
                      for kc in range(8):
                          mm(B[bk][:, 0:256], xT[:, kc, t_ * 128:(t_ + 1) * 128], Wv_[:, kc, 0:256], kc == 0, kc == 7,
                             [Wv_k, ("xT", t_)], [Bk[bk]])
                      cp("act", vA[l][:, tg, :, 0:64], B[bk][:, 0:256].rearrange("p (h d) -> p h d", d=64), [Bk[bk]],
                         [("vA", l, tg)])
                  biasv = biasb.v("p (h t q) -> p h t q", h=4, t=5)
                  qv = qTa.v("p (a t) -> p a t", a=2)
                  its = []
                  for t_ in range(NT):
                      tq = blk * 4 + t_
                      for h in range(4):
                          kts = list(range(max(0, tq - 4), tq + 1))
                          for kt in kts:
                              its.append((t_, tq, h, kt, kt == kts[0], kt == kts[-1], h == 3 and kt == kts[-1]))

                  def att_score(i):
                      t_, tq, h, kt, first, last, fin = its[i]
                      hp, po = h // 2, (h % 2) * 64
                      sbk = 4 + (h % 2) + 2 * (i % 2)
                      kr = kt % 8
                      mm(B[sbk][:, 0:128], kTa[l][po:po + 64, hp, kr * 128:(kr + 1) * 128],
                         qv[po:po + 64, hp, t_ * 128:(t_ + 1) * 128], True, False,
                         [("kTa", l, kr)] + qTa.keys, [Bk[sbk]])
                      mm(B[sbk][:, 0:128], identbf[:], biasv[:, h, tq - kt, :], False, True,
                         ["identbf"] + biasb.keys, [Bk[sbk]])

                  def att_pv(i):
                      t_, tq, h, kt, first, last, fin = its[i]
                      sbk = 4 + (h % 2) + 2 * (i % 2)
                      kr = kt % 8
                      pT = pTs[i % 3]
                      POv = B[t_ % 2][:, 0:260].rearrange("p (h d) -> p h d", d=65)
                      POk = Bk[t_ % 2]
                      act(pT.ap[:, 0:128], B[sbk][:, 0:128], AF.Exp, [Bk[sbk]], pT.keys)
                      mm(POv[:, h, :], pT.ap[:, 0:128], vA[l][:, kr, h, :], first, last, pT.keys + [("vA", l, kr)], [POk])
                      if fin:
                          mk.op("dve", lambda e, POv=POv: e.reciprocal(out=rc[:], in_=POv[:, :, 64:65]), reads=[POk], writes=["rc"])
                          tt("dve", yatt.ap[:, 0:256].rearrange("p (h d) -> p h d", d=64), POv[:, :, 0:64],
                             rc[:].to_broadcast([128, 4, 64]), ALU.mult, [POk, "rc"], yatt.keys)
                          tb = 2 + t_ % 2
                          Bb = B[tb][:].bitcast(BF16)
                          for hp in range(2):
                              tr(Bb[:, hp * 128:(hp + 1) * 128], yatt.ap[:, hp * 128:(hp + 1) * 128], identbf[:],
                                 yatt.keys + ["identbf"], [Bk[tb]])
                          cp("act", yT[:, 6:8, t_ * 128:(t_ + 1) * 128], Bb[:, 0:256].rearrange("p (a b) -> p a b", b=128),
                             [Bk[tb]], [("yT", 6), ("yT", 7)])

                  att_score(0)
                  for i in range(len(its)):
                      if i + 1 < len(its):
                          att_score(i + 1)
                      att_pv(i)

                if "ret" in STAGES:
                  ar.reset()
                  cosb = ar.f32(1)
                  sinb = ar.f32(1)
                  q32s = [ar.f32(1) for _ in range(2)]
                  tA = [ar.f32(1) for _ in range(2)]
                  tBm = [ar.f32(1) for _ in range(2)]
                  qrT = [ar.bf(1) for _ in range(3)]
                  krT = [ar.bf(1) for _ in range(3)]
                  qdT = [ar.bf(1) for _ in range(3)]
                  kdT = [ar.bf(1) for _ in range(3)]
                  sg = [ar.bf(1) for _ in range(3)]
                  kdtok = ar.bf(3)
                  v_r = ar.bf(3)
                  scms = [ar.bf(2) for _ in range(4)]
                  Rsnap = ar.bf(3)
                  yr = [ar.f32(1) for _ in range(3)]
                  hsc = [ar.f32(1) for _ in range(3)]
                  mk.dma("sp", cosb.ap, cos_d[:, blk * TB:(blk + 1) * TB], writes=cosb.keys)
                  mk.dma("sp", sinb.ap, sin_d[:, blk * TB:(blk + 1) * TB], writes=sinb.keys)
                  Wq, Wq_k = wload(w_in_d[l][:, OFF_RET:OFF_RET + 384], 384)
                  Wk, Wk_k = wload(w_in_d[l][:, OFF_RET + 384:OFF_RET + 768], 384)
                  Wg, Wg_k = wload(w_in_d[l][:, OFF_RET + 1152:OFF_RET + 1536], 384)
                  qq = 0
                  for hp in range(3):
                      for (Wx, Wx_k, scl, dst, decidx, dst2) in ((Wq, Wq_k, 1.0, qrT[hp], 1, qdT[hp]),
                                                                 (Wk, Wk_k, 0.125, krT[hp], 0, kdT[hp])):
                          i2 = qq % 2
                          qq += 1
                          pz = i2 * 2
                          for kc in range(8):
                              mm(B[pz][:], Wx[:, kc, hp * 128:(hp + 1) * 128], xT[:, kc, :], kc == 0, kc == 7,
                                 [Wx_k] + XTK, [Bk[pz]])
                          act(q32s[i2].ap, B[pz][:], AF.Copy, [Bk[pz]], q32s[i2].keys, scale=scl)
                          mm(B[pz + 1][:], prot32[:], q32s[i2].ap, True, True, q32s[i2].keys + ["prot32"], [Bk[pz + 1]])
                          tt("dve", tA[i2].ap, q32s[i2].ap, cosb.ap, ALU.mult, q32s[i2].keys + cosb.keys, tA[i2].keys)
                          tt("dve", tBm[i2].ap, B[pz + 1][:], sinb.ap, ALU.mult, [Bk[pz + 1]] + sinb.keys, tBm[i2].keys)
                          tt("dve", tA[i2].ap, tA[i2].ap, tBm[i2].ap, ALU.add, tA[i2].keys + tBm[i2].keys, tA[i2].keys)
                          cp("act", dst.ap, tA[i2].ap, tA[i2].keys, dst.keys)
                          tt("dve", dst2.v("p (c j) -> p c j", j=64), tA[i2].v("p (c j) -> p c j", j=64),
                             kdq[:, decidx, hp:hp + 1, :].to_broadcast([128, 8, 64]), ALU.mult,
                             tA[i2].keys + ["kdq"], dst2.keys)
                      for kc in range(8):
                          mm(B[4][:], Wg[:, kc, hp * 128:(hp + 1) * 128], xT[:, kc, :], kc == 0, kc == 7,
                             [Wg_k] + XTK, [Bk[4]])
                      act(sg[hp].ap, B[4][:], AF.Silu, [Bk[4]], sg[hp].keys)
                  mk.muted = mk.muted or CUT == "ret1"
                  Wv_, Wv_k = wload(w_in_d[l][:, OFF_RET + 768:OFF_RET + 1152], 384)
                  vrv = v_r.v("p (t c) -> p t c", t=4)
                  kdv = kdtok.v("p (t c) -> p t c", t=4)
                  for t_ in range(NT):
                      bk = 5 + t_ % 2
                      for kc in range(8):
                          mm(B[bk][:, 0:384], xT[:, kc, t_ * 128:(t_ + 1) * 128], Wv_[:, kc, 0:384], kc == 0, kc == 7,
                             [Wv_k, ("xT", t_)], [Bk[bk]])
                      cp("dve", vrv[:, t_, :], B[bk][:, 0:384], [Bk[bk]], v_r.keys)
                      Bb = B[7][:].bitcast(BF16)
                      for hp in range(3):
                          tr(Bb[:, hp * 128:(hp + 1) * 128], kdT[hp].ap[:, t_ * 128:(t_ + 1) * 128], identbf[:],
                             kdT[hp].keys + ["identbf"], [Bk[7]])
                      cp("act", kdv[:, t_, :], Bb[:, 0:384], [Bk[7]], kdtok.keys)
                  mk.muted = mk.muted or CUT == "ret2"
                  rsv = Rsnap.v("p (c x) -> p c x", c=8)
                  Rk = f"R32_{l}"

                  def gen_p1():
                      for c in range(8):
                          t_, tp = c // 2, (c % 2) * 64
                          bu = c % 2
                          for h in range(6):
                              hp, po = h // 2, (h % 2) * 64
                              mm(B[bu][po:po + 64, hp * 64:(hp + 1) * 64], kdv[tp:tp + 64, t_, h * 64:(h + 1) * 64],
                                 vrv[tp:tp + 64, t_, h * 64:(h + 1) * 64], True, True, kdtok.keys + v_r.keys, [Bk[bu]])
                          yield
                          cp("act", rsv[:, c, :], R32[l][:].rearrange("p a b -> p (a b)"), [Rk], Rsnap.keys)
                          tt("dve", R32[l][:], R32[l][:], cdt[:].to_broadcast([128, 3, 64]), ALU.mult, [Rk, "cdt"], [Rk])
                          tt("dve", R32[l][:], R32[l][:], B[bu][:, 0:192].rearrange("p (a b) -> p a b", b=64), ALU.add,
                             [Rk, Bk[bu]], [Rk])
                          yield

                  def gen_sc():
                      for t_ in range(NT):
                          cs_ = slice(t_ * 128, (t_ + 1) * 128)
                          for hh in range(2):
                              po = hh * 64
                              for hp in range(3):
                                  mm(B[2 + hh][:, hp * 128:(hp + 1) * 128], krT[hp].ap[po:po + 64, cs_], qrT[hp].ap[po:po + 64, cs_],
                                     True, True, krT[hp].keys + qrT[hp].keys, [Bk[2 + hh]])
                          yield
                          for hh in range(2):
                              tt("dve", scms[t_].ap[:, hh * 384:(hh + 1) * 384], B[2 + hh][:, 0:384], dmatT[:, hh * 384:(hh + 1) * 384],
                                 ALU.mult, [Bk[2 + hh], "dmatT"], scms[t_].keys)
                          yield

                  run_interleaved([gen_p1(), gen_sc()])
                  mk.muted = mk.muted or CUT == "ret3"
                  for t_ in range(NT):
                      cs_ = slice(t_ * 128, (t_ + 1) * 128)
                      scm = scms[t_]
                      scv = scm.v("p (h i) -> p h i", h=8)
                      pyb = 4 + 2 * (t_ % 2)
                      for hh in range(2):
                          po = hh * 64
                          py = pyb + hh
                          for hp in range(3):
                              c0_ = hp * 128
                              h = 2 * hp + hh
                              mm(B[py][po:po + 64, c0_:c0_ + 128], vrv[:, t_, h * 64:(h + 1) * 64], scv[:, hh * 3 + hp, :], True, False,
                                 v_r.keys + scm.keys, [Bk[py]])
                              for hf in range(2):
                                  c = 2 * t_ + hf
                                  mm(B[py][po:po + 64, c0_ + hf * 64:c0_ + (hf + 1) * 64],
                                     rsv[po:po + 64, c, hp * 64:(hp + 1) * 64],
                                     qdT[hp].ap[po:po + 64, t_ * 128 + hf * 64:t_ * 128 + (hf + 1) * 64], False, hf == 1,
                                     Rsnap.keys + qdT[hp].keys, [Bk[py]])
                      for hh in range(2):
                          po = hh * 64
                          py = pyb + hh
                          for hp in range(3):
                              cp("act", yr[hp].ap[po:po + 64, cs_], B[py][po:po + 64, hp * 128:(hp + 1) * 128], [Bk[py]], yr[hp].keys)
                  mk.muted = mk.muted or CUT == "ret4"
                  headnorm3(yr, 2, 32, 35, hsc, l)
                  for hp in range(3):
                      tt("dve", yT[:, 3 + hp, :], yr[hp].ap, sg[hp].ap, ALU.mult, yr[hp].keys + sg[hp].keys,
                         [("yT", 3 + hp)])

                mk.muted = False
                if "rwkv" in STAGES:
                  ar.reset()
                  z32 = []
                  for _ in range(2):
                      zb_ = ar.bf(3)
                      z32.append(Buf(zb_.ap.bitcast(F32), zb_.keys))
                  t1 = [ar.f32(1) for _ in range(2)]
                  f = [ar.f32(1) for _ in range(10)]
                  f2 = [ar.f32(1) for _ in range(5)]
                  lin = ar.bf(1)
                  sgl = ar.bf(1)
                  sqb = ar.bf(1)
                  rkb = ar.bf(1)
                  bhb = ar.bf(1)
                  khb = ar.bf(1)
                  vbb = ar.bf(1)
                  artT = [ar.bf(2) for _ in range(3)]
                  btT = [ar.bf(1) for _ in range(3)]
                  ktT = [ar.bf(1) for _ in range(3)]
                  bonus = [ar.bf(1) for _ in range(3)]
                  gT = [ar.bf(1) for _ in range(3)]
                  tokm = ar.bf(9)
                  LM = [ar.bf(2) for _ in range(3)]
                  LT = [ar.bf(1) for _ in range(3)]
                  Nm = [ar.bf(1) for _ in range(3)]
                  XX = [[ar.bf(1) for _ in range(2)] for _ in range(3)]
                  rhs_sb = ar.bf(1)
                  sa_sb = ar.bf(1)
                  Sbf = [ar.bf(1) for _ in range(2)]
                  halk = f"hal{l}"
                  sh = [0]

                  def shift(pbk, i, out_ap, out_keys):
                      j = sh[0] % 2
                      sh[0] += 1
                      z = z32[j]
                      cp("act", z.ap[:, 1:513], B[pbk][:], [Bk[pbk]], z.keys)
                      cp("act", z.ap[:, 0:1], hal[l][:, i:i + 1], [halk], z.keys)
                      act(t1[j].ap, B[pbk][:], AF.Identity, [Bk[pbk]] + PPk, t1[j].keys, scale=omm[:, l, i:i + 1])
                      stt(out_ap, z.ap[:, 0:512], P(i), t1[j].ap, ALU.mult, ALU.add, z.keys + t1[j].keys + PPk, out_keys)
                      cp("act", hal[l][:, i:i + 1], z.ap[:, 512:513], z.keys, [halk])

                  Wl, Wl_k = wload(w_in_d[l][:, 1152:1408], 256)
                  Wr, Wr_k = wload(w_in_d[l][:, 0:384], 384)
                  Wk, Wk_k = wload(w_in_d[l][:, 384:768], 384)
                  zs = f[9]
                  for kc in range(8):
                      mm(B[0][:], Wl[:, kc, 0:128], xT[:, kc, :], kc == 0, kc == 7, [Wl_k] + XTK, [Bk[0]])
                  shift(0, 9, zs.ap, zs.keys)
                  act(lin.ap[0:64, :], zs.ap[0:64, :], AF.Tanh, zs.keys, lin.keys)
                  cp("dve", lin.ap[64:128, :], zs.ap[64:128, :], zs.keys, lin.keys)
                  for kc in range(8):
                      mm(B[1][:], Wl[:, kc, 128:256], xT[:, kc, :], kc == 0, kc == 7, [Wl_k] + XTK, [Bk[1]])
                  shift(1, 10, zs.ap, zs.keys)
                  act(sgl.ap, zs.ap, AF.Sigmoid, zs.keys, sgl.keys)
                  Wv_, Wv_k = None, None
                  tokv = tokm.v("p (t q c) -> p t q c", t=4, q=3)
                  Wv_, Wv_k = wload(w_in_d[l][:, 768:1152], 384)

                  def proj_stage(hp_):
                      r_, k_, v_ = (f2[2], f2[3], f2[4]) if hp_ % 2 == 1 else (f[6], f[7], f[8])
                      hs_ = slice(hp_ * 128, (hp_ + 1) * 128)
                      for (Wx, Wx_k, pbk, i, dst) in ((Wr, Wr_k, 5, hp_, r_), (Wk, Wk_k, 6, 3 + hp_, k_), (Wv_, Wv_k, 7, 6 + hp_, v_)):
                          for kc in range(8):
                              mm(B[pbk][:], Wx[:, kc, hs_], xT[:, kc, :], kc == 0, kc == 7, [Wx_k] + XTK, [Bk[pbk]])
                          shift(pbk, i, dst.ap, dst.keys)

                  kkn2 = ar.f32(1)

                  def bufs(hp_):
                      lw, a32, E, Einv, Epv, Dd, r32, k32, v32, kkn = f[0:10]
                      if hp_ % 2 == 1:
                          lw, a32, r32, k32, v32 = f2
                          kkn = kkn2
                      return lw, a32, E, Einv, Epv, Dd, r32, k32, v32, kkn

                  def s2a(hp):
                      lw, a32, E, Einv, Epv, Dd, r32, k32, v32, kkn = bufs(hp)
                      hs = slice(hp * 128, (hp + 1) * 128)
                      mm(B[2][:], lorab[0:64, l, hs], lin.ap[0:64, :], True, True, ["lorab"] + lin.keys, [Bk[2]])
                      act(lw.ap, B[2][:], AF.Sigmoid, [Bk[2]] + PPk, lw.keys, bias=P(11 + hp))
                      mm(B[3][:], lorab[64:128, l, hs], lin.ap[64:128, :], True, True, ["lorab"] + lin.keys, [Bk[3]])
                      act(a32.ap, B[3][:], AF.Sigmoid, [Bk[3]] + PPk, a32.keys, bias=P(14 + hp))
                      mm(B[4][:], gup[:, l, hs], sgl.ap, True, True, ["gup"] + sgl.keys, [Bk[4]])
                      cp("act", gT[hp].ap, B[4][:], [Bk[4]], gT[hp].keys)
                      act(sqb.ap, k32.ap, AF.Square, k32.keys + PPk, sqb.keys, scale=P(17 + hp))
                      mm(B[2][:], bonesbf[:], sqb.ap, True, True, sqb.keys + ["bonesbf"], [Bk[2]])
                      act(kkn.ap, B[2][:], AF.Sqrt, [Bk[2]], kkn.keys)

                  def s2b(hp):
                      lw, a32, E, Einv, Epv, Dd, r32, k32, v32, kkn = bufs(hp)
                      mk.op("dve", lambda e, E=E, lw=lw: e.tensor_tensor_scan(out=E.ap, data0=cmask[:], data1=lw.ap, initial=0.0,
                                                                             op0=ALU.mult, op1=ALU.add),
                            reads=["cmask"] + lw.keys, writes=E.keys)
                      tt("dve", Epv.ap, E.ap, lw.ap, ALU.subtract, E.keys + lw.keys, Epv.keys)
                      csv = E.v("p (c j) -> p c j", j=64)
                      tt("dve", Dd.v("p (c j) -> p c j", j=64), csv[:, :, 63:64].to_broadcast([128, 8, 64]), csv, ALU.subtract,
                         E.keys, Dd.keys)
                      act(Einv.ap, E.ap, AF.Exp, E.keys, Einv.keys, scale=C0)
                      act(E.ap, E.ap, AF.Exp, E.keys, E.keys, scale=-C0)
                      act(Epv.ap, Epv.ap, AF.Exp, Epv.keys, Epv.keys, scale=-C0)
                      act(Dd.ap, Dd.ap, AF.Exp, Dd.keys, Dd.keys, scale=-C0)
                      ts("dve", kkn.ap, kkn.ap, 1e-12, None, ALU.max, None, kkn.keys, kkn.keys)
                      mk.op("dve", lambda e, kkn=kkn: e.reciprocal(out=kkn.ap, in_=kkn.ap), reads=kkn.keys, writes=kkn.keys)
                      stt(kkn.ap, k32.ap, P(17 + hp), kkn.ap, ALU.mult, ALU.mult, k32.keys + kkn.keys + PPk, kkn.keys)
                      kmod = lw
                      ts("dve", kmod.ap, a32.ap, P(20 + hp), omka[:, l, hp:hp + 1], ALU.mult, ALU.add, a32.keys + PPk, kmod.keys)
                      tt("dve", kmod.ap, kmod.ap, k32.ap, ALU.mult, kmod.keys + k32.keys, kmod.keys)
                      bvec = a32
                      tt("dve", bvec.ap, kkn.ap, a32.ap, ALU.mult, kkn.keys + a32.keys, bvec.keys)
                      cp("dve", pcs[:, hp, :], E.v("p (c j) -> p c j", j=64)[:, :, 63], E.keys, [("pcs", hp)])
                      arv = artT[hp].v("p (t q j) -> p t q j", t=4, q=2)
                      stt(arv[:, :, 0, :], kkn.v("p (t j) -> p t j", j=128), -1.0, Epv.v("p (t j) -> p t j", j=128),
                          ALU.mult, ALU.mult, kkn.keys + Epv.keys, artT[hp].keys)
                      tt("dve", arv[:, :, 1, :], r32.v("p (t j) -> p t j", j=128), E.v("p (t j) -> p t j", j=128), ALU.mult,
                         r32.keys + E.keys, artT[hp].keys)
                      tt("dve", btT[hp].ap, bvec.ap, Einv.ap, ALU.mult, bvec.keys + Einv.keys, btT[hp].keys)
                      tt("dve", ktT[hp].ap, kmod.ap, Einv.ap, ALU.mult, kmod.keys + Einv.keys, ktT[hp].keys)
                      tt("dve", bhb.ap, bvec.ap, Dd.ap, ALU.mult, bvec.keys + Dd.keys, bhb.keys)
                      tt("dve", khb.ap, kmod.ap, Dd.ap, ALU.mult, kmod.keys + Dd.keys, khb.keys)
                      cp("act", vbb.ap, v32.ap, v32.keys, vbb.keys)
                      stt(rkb.ap, r32.ap, P(23 + hp), kmod.ap, ALU.mult, ALU.mult, r32.keys + kmod.keys + PPk, rkb.keys)

                  def s3(hp):
                      lw, a32, E, Einv, Epv, Dd, r32, k32, v32, kkn = bufs(hp)
                      hs = slice(hp * 128, (hp + 1) * 128)
                      mm(B[3][:], bonesbf[:], rkb.ap, True, True, rkb.keys + ["bonesbf"], [Bk[3]])
                      tt("dve", bonus[hp].ap, B[3][:], v32.ap, ALU.mult, [Bk[3]] + v32.keys, bonus[hp].keys)
                      for t_ in range(NT):
                          Bb = B[4][:].bitcast(BF16)
                          for qi, src in enumerate((bhb, khb, vbb)):
                              tr(Bb[:, qi * 128:(qi + 1) * 128], src.ap[:, t_ * 128:(t_ + 1) * 128], identbf[:],
                                 src.keys + ["identbf"], [Bk[4]])
                          cp("act", tokv[:, t_, :, hs], Bb[:, 0:384].rearrange("p (q c) -> p q c", q=3), [Bk[4]], tokm.keys)

                  proj_stage(0)
                  s2a(0)
                  proj_stage(1)
                  s2a(1)
                  s2b(0)
                  s3(0)
                  s2b(1)
                  proj_stage(2)
                  s2a(2)
                  s3(1)
                  s2b(2)
                  s3(2)

                  mk.muted = mk.muted or CUT == "rw1"
                  yw = f[0:3]
                  Sk = f"S32_{l}"
                  for Sb_ in Sbf:
                      memset("dve", Sb_.ap, 0.0, Sb_.keys)
                  memset("dve", rhs_sb.ap, 0.0, rhs_sb.keys)
                  memset("dve", sa_sb.ap, 0.0, sa_sb.keys)

                  def put_state(dst):
                      dv = dst.ap[:, 0:384].rearrange("p (a h b) -> p a h b", a=3, h=2)
                      for hh in range(2):
                          po = hh * 64
                          cp("act", dv[po:po + 64, :, hh, :], S32[l][po:po + 64, :, :], [Sk], dst.keys)

                  put_state(Sbf[0])
                  def alias_bf(buf, n0, n):
                      s0 = buf.keys[0][1] + n0
                      return Buf(A[:, s0:s0 + n, :].rearrange("p a b -> p (a b)"), [("A", q_) for q_ in range(s0, s0 + n)])

                  LMs = [LM, [alias_bf(f[6], 0, 2), alias_bf(f[7], 0, 2), alias_bf(f[8], 0, 2)]]
                  LTs = [LT, [alias_bf(f[9], 0, 1), alias_bf(f[9], 1, 1), alias_bf(f2[0], 0, 1)]]
                  Nms = [Nm, [alias_bf(f2[0], 1, 1), alias_bf(f2[1], 0, 1), alias_bf(f2[1], 1, 1)]]

                  def mat(t_):
                      LM_, LT_, Nm_ = LMs[t_ % 2], LTs[t_ % 2], Nms[t_ % 2]
                      cs_ = slice(t_ * 128, (t_ + 1) * 128)
                      for hp in range(3):
                          arv = artT[hp].v("p (t q j) -> p t q j", t=4, q=2)
                          for hh in range(2):
                              po = hh * 64
                              ar_rhs = arv[po:po + 64, t_, :, :].rearrange("p q j -> p (q j)")
                              mm(B[hh][:, 0:256], btT[hp].ap[po:po + 64, cs_], ar_rhs, True, True,
                                 btT[hp].keys + artT[hp].keys, [Bk[hh]])
                              mm(B[hh][:, 256:512], ktT[hp].ap[po:po + 64, cs_], ar_rhs, True, True,
                                 ktT[hp].keys + artT[hp].keys, [Bk[hh]])
                              mm(B[2 + hh][:, 0:128], arv[po:po + 64, t_, 0, :], btT[hp].ap[po:po + 64, cs_],
                                 True, True, btT[hp].keys + artT[hp].keys, [Bk[2 + hh]])
                          for hh in range(2):
                              tt("dve", LM_[hp].ap[:, hh * 512:(hh + 1) * 512], B[hh][:], msk4[:], ALU.mult, [Bk[hh], "msk4"], LM_[hp].keys)
                              tt("dve", LT_[hp].ap[:, hh * 128:(hh + 1) * 128], B[2 + hh][:, 0:128], mskL[:, 0, :], ALU.mult,
                                 [Bk[2 + hh], "mskL"], LT_[hp].keys)
                          lmv = LM_[hp].v("p (h q j) -> p h q j", h=2, q=4)
                          nv = Nm_[hp].ap[:, 0:256].rearrange("p (h j) -> p h j", h=2)
                          tt("dve", nv, lmv[:, :, 0, :], identbf[:].rearrange("p (o j) -> p o j", o=1).to_broadcast([128, 2, 128]),
                             ALU.add, LM_[hp].keys + ["identbf"], Nm_[hp].keys)

                  def gen_dbl(t_):
                      LM_, LT_, Nm_ = LMs[t_ % 2], LTs[t_ % 2], Nms[t_ % 2]
                      Xc, XTc, Xk = [], [], []
                      for hp in range(3):
                          lmv = LM_[hp].v("p (h q j) -> p h q j", h=2, q=4)
                          Xc.append([lmv[:, 0, 0, :], lmv[:, 1, 0, :]])
                          XTc.append([LT_[hp].ap[:, 0:128], LT_[hp].ap[:, 128:256]])
                          Xk.append(LM_[hp].keys + LT_[hp].keys)
                      for lev in range(1, 6):
                          for hp in range(3):
                              bx = 4 + hp
                              bxv = B[bx][:].rearrange("p (h q j) -> p h q j", h=2, q=2)
                              for hh in range(2):
                                  if lev < 5:
                                      mm(bxv[:, hh, 0, :], XTc[hp][hh], Xc[hp][hh], True, True, Xk[hp], [Bk[bx]])
                                  mm(bxv[:, hh, 1, :], Xc[hp][hh], XTc[hp][hh], True, True, Xk[hp], [Bk[bx]])
                          yield
                          for hp in range(3):
                              bx = 4 + hp
                              xx = XX[hp][lev % 2]
                              xxv = xx.v("p (h q j) -> p h q j", h=2, q=2)
                              bxv = B[bx][:].rearrange("p (h q j) -> p h q j", h=2, q=2)
                              eng = "dve" if hp == 1 else "act"
                              if lev < 5:
                                  cp(eng, xx.ap, B[bx][:], [Bk[bx]], xx.keys)
                              else:
                                  cp(eng, xxv[:, :, 1, :], bxv[:, :, 1, :], [Bk[bx]], xx.keys)
                              Xc[hp] = [xxv[:, 0, 0, :], xxv[:, 1, 0, :]]
                              XTc[hp] = [xxv[:, 0, 1, :], xxv[:, 1, 1, :]]
                              Xk[hp] = xx.keys
                          yield
                          for hp in range(3):
                              bn = 4 + hp
                              nv = Nm_[hp].ap[:, 0:256].rearrange("p (h j) -> p h j", h=2)
                              for hh in range(2):
                                  mm(B[bn][:, hh * 128:(hh + 1) * 128], XTc[hp][hh], nv[:, hh, :], True, True,
                                     Xk[hp] + Nm_[hp].keys, [Bk[bn]])
                          yield
                          for hp in range(3):
                              bn = 4 + hp
                              nv = Nm_[hp].ap[:, 0:256].rearrange("p (h j) -> p h j", h=2)
                              tt("dve", nv, nv, B[bn][:, 0:256].rearrange("p (h j) -> p h j", h=2), ALU.add,
                                 Nm_[hp].keys + [Bk[bn]], Nm_[hp].keys)
                          yield

                  ccs = [0]

                  def gen_seq(t_):
                      LM_, LT_, Nm_ = LMs[t_ % 2], LTs[t_ % 2], Nms[t_ % 2]
                      for hf in range(2):
                          c = 2 * t_ + hf
                          tp = hf * 64
                          tsl = slice(tp, tp + 64)
                          Sc, Sn = Sbf[ccs[0] % 2], Sbf[(ccs[0] + 1) % 2]
                          ccs[0] += 1
                          sxv = Sc.ap[:, 0:384].rearrange("p (a h b) -> p a h b", a=3, h=2)
                          for h in range(6):
                              hp, hh = h // 2, h % 2
                              lmv = LM_[hp].v("p (h q j) -> p h q j", h=2, q=4)
                              arv = artT[hp].v("p (t q j) -> p t q j", t=4, q=2)
                              o_ = B[7][tsl, h * 64:(h + 1) * 64]
                              mm(o_, lmv[:, hh, 2, tsl], tokv[:, t_, 2, h * 64:(h + 1) * 64], True, False,
                                 LM_[hp].keys + tokm.keys, [Bk[7]])
                              mm(o_, arv[:, t_, 0, tsl], sxv[:, hp, hh, :], False, True,
                                 artT[hp].keys + Sc.keys, [Bk[7]])
                          yield
                          cp("act", rhs_sb.ap[tsl, 0:384], B[7][tsl, 0:384], [Bk[7]], rhs_sb.keys)
                          yield
                          for h in range(6):
                              hp, hh = h // 2, h % 2
                              nv = Nm_[hp].ap[:, 0:256].rearrange("p (h j) -> p h j", h=2)
                              mm(B[7][tsl, h * 64:(h + 1) * 64], nv[:, hh, tsl], rhs_sb.ap[:, h * 64:(h + 1) * 64], True, True,
                                 Nm_[hp].keys + rhs_sb.keys, [Bk[7]])
                          yield
                          cp("dve", sa_sb.ap[tsl, 0:384], B[7][tsl, 0:384], [Bk[7]], sa_sb.keys)
                          yield
                          bsn = 2 + hf
                          for h in range(6):
                              hp, hh = h // 2, h % 2
                              po = hh * 64
                              o_ = B[bsn][po:po + 64, hp * 64:(hp + 1) * 64]
                              mm(o_, tokv[tsl, t_, 1, h * 64:(h + 1) * 64], tokv[tsl, t_, 2, h * 64:(h + 1) * 64], True, False,
                                 tokm.keys, [Bk[bsn]])
                              mm(o_, tokv[tsl, t_, 0, h * 64:(h + 1) * 64], sa_sb.ap[tsl, h * 64:(h + 1) * 64], False, True,
                                 tokm.keys + sa_sb.keys, [Bk[bsn]])
                          yield
                          for h in range(6):
                              hp, hh = h // 2, h % 2
                              po = hh * 64
                              lmv = LM_[hp].v("p (h q j) -> p h q j", h=2, q=4)
                              arv = artT[hp].v("p (t q j) -> p t q j", t=4, q=2)
                              o_ = B[0][po:po + 64, hp * 64:(hp + 1) * 64]
                              mm(o_, sxv[:, hp, hh, :], arv[:, t_, 1, tsl], True, False, Sc.keys + artT[hp].keys, [Bk[0]])
                              mm(o_, sa_sb.ap[:, h * 64:(h + 1) * 64], lmv[:, hh, 1, tsl], False, False,
                                 sa_sb.keys + LM_[hp].keys, [Bk[0]])
                              mm(o_, tokv[:, t_, 2, h * 64:(h + 1) * 64], lmv[:, hh, 3, tsl], False, True,
                                 tokm.keys + LM_[hp].keys, [Bk[0]])
                          yield
                          for hp in range(3):
                              stt(S32[l][:, hp, :], S32[l][:, hp, :], pcs[:, hp, c:c + 1], B[bsn][:, hp * 64:(hp + 1) * 64],
                                  ALU.mult, ALU.add, [Sk, ("pcs", hp), Bk[bsn]], [Sk])
                          put_state(Sn)
                          yield
                          for hp in range(3):
                              cp("act", yw[hp].ap[:, c * 64:(c + 1) * 64], B[0][:, hp * 64:(hp + 1) * 64], [Bk[0]], yw[hp].keys)
                          yield

                  mat(0)
                  run_interleaved([gen_dbl(0)])
                  for t_ in range(NT):
                      if t_ + 1 < NT:
                          mat(t_ + 1)
                          run_interleaved([gen_seq(t_), gen_dbl(t_ + 1)])
                      else:
                          run_interleaved([gen_seq(t_)])
                  mk.muted = mk.muted or CUT == "rw3"
                  headnorm3(yw, 1, 26, 29, f[3:6], l)
                  for hp in range(3):
                      tt("dve", yw[hp].ap, yw[hp].ap, bonus[hp].ap, ALU.add, yw[hp].keys + bonus[hp].keys, yw[hp].keys)
                  for hp in range(3):
                      tt("dve", yT[:, hp, :], yw[hp].ap, gT[hp].ap, ALU.mult, yw[hp].keys + gT[hp].keys, [("yT", hp)])
                  if s == 0 and blk == 0 and l == 0:
                      tap("yT", yT[:].rearrange("p a b -> p (a b)"), YTK)

                mk.muted = False
                if "out" in STAGES:
                  load_ln(1 + l)
                  Wo = [wload(w_out_d[l][:, hf * 512:(hf + 1) * 512], 512) for hf in range(2)]
                  for t_ in range(NT):
                      sb_ = s32[t_ % 2]
                      sk = f"s32_{t_ % 2}"
                      for hf in range(2):
                          bk = 2 * (t_ % 2) + hf
                          for kc in range(8):
                              mm(B[bk][:], yT[:, kc, t_ * 128:(t_ + 1) * 128], Wo[hf][0][:, kc, :], kc == 0, kc == 7,
                                 [("yT", kc), Wo[hf][1]], [Bk[bk]])
                          stt(sb_[:, hf * 512:(hf + 1) * 512], xres[:, t_, hf * 512:(hf + 1) * 512], ALPHA, B[bk][:],
                              ALU.mult, ALU.add, [("xres", t_), Bk[bk]], [sk])
                      ln_a(t_, sb_[:], sk)
                      if t_ >= 1:
                          ln_t(t_ - 1, 4 + 2 * ((t_ - 1) % 2))
                  ln_t(NT - 1, 4 + 2 * ((NT - 1) % 2))

                if "ffn" in STAGES:
                  ar.reset()
                  hT = ar.bf(NJ)
                  u32 = [ar.f32(2) for _ in range(4)]
                  c32 = [ar.f32(1) for _ in range(4)]
                  hTv = hT.v("p (j t) -> p j t", j=NJ)
                  chk = f"chal{l}"
                  ui = 0
                  for jb in range(6):
                      ncol = 512 if jb < 5 else 256
                      Wg_, Wg_k = wload(w_up_d[l][:, jb * 512:jb * 512 + ncol], ncol)
                      Wv2, Wv2_k = wload(w_up_d[l][:, DFF + jb * 512:DFF + jb * 512 + ncol], ncol)
                      for jj in range(ncol // 128):
                          j = jb * 4 + jj
                          cres = []
                          for (Wx, Wx_k, cidx) in ((Wg_, Wg_k, j), (Wv2, Wv2_k, NJ + j)):
                              k2 = ui % 4
                              ui += 1
                              pbk = k2
                              for kc in range(8):
                                  mm(B[pbk][:], Wx[:, kc, jj * 128:(jj + 1) * 128], xT[:, kc, :], kc == 0, kc == 7,
                                     [Wx_k] + XTK, [Bk[pbk]])
                              u = u32[k2]
                              c_ = c32[k2]
                              cp("act", u.ap[:, 2:514], B[pbk][:], [Bk[pbk]], u.keys)
                              cp("act", u.ap[:, 0:2], chal[l][:, cidx, :], [chk], u.keys)
                              act(c_.ap, B[pbk][:], AF.Identity, [Bk[pbk], "cw", "cb"], c_.keys, bias=cb[:, l, cidx:cidx + 1],
                                  scale=cw[:, l, cidx, 2:3])
                              stt(c_.ap, u.ap[:, 1:513], cw[:, l, cidx, 1:2], c_.ap, ALU.mult, ALU.add, u.keys + c_.keys + ["cw"], c_.keys)
                              stt(c_.ap, u.ap[:, 0:512], cw[:, l, cidx, 0:1], c_.ap, ALU.mult, ALU.add, u.keys + c_.keys + ["cw"], c_.keys)
                              cp("act", chal[l][:, cidx, :], u.ap[:, 512:514], u.keys, [chk])
                              cres.append(c_)
                          act(cres[0].ap, cres[0].ap, AF.Silu, cres[0].keys, cres[0].keys)
                          tt("dve", hTv[:, j, :], cres[0].ap, cres[1].ap, ALU.mult, cres[0].keys + cres[1].keys, [hT.keys[j]])
                  load_ln(1 + L + l)
                  first = True
                  for (j0, nj) in ((0, 8), (8, 8), (16, 6)):
                      for hf in range(2):
                          Wd, Wd_k = wload(w_dn_d[l][j0 * 128:(j0 + nj) * 128, hf * 512:(hf + 1) * 512], 512, nk=nj)
                          for jj in range(nj):
                              j = j0 + jj
                              for t_ in range(NT):
                                  bk = 2 * t_ + hf
                                  mm(B[bk][:], hTv[:, j, t_ * 128:(t_ + 1) * 128], Wd[:, jj, :], j == 0, j == NJ - 1,
                                     [hT.keys[j], Wd_k], [Bk[bk]])
                  for t_ in range(NT):
                      sb_ = s32[t_ % 2]
                      sk = f"s32_{t_ % 2}"
                      for hf in range(2):
                          bk = 2 * t_ + hf
                          stt(sb_[:, hf * 512:(hf + 1) * 512], xres[:, t_, hf * 512:(hf + 1) * 512], ALPHA, B[bk][:],
                              ALU.mult, ALU.add, [("xres", t_), Bk[bk]], [sk])
                      ln_a(t_, sb_[:], sk)
                      if t_ >= 1:
                          ln_t(t_ - 1, 2 * (t_ - 1))
                  ln_t(NT - 1, 2 * (NT - 1))
            oview = out_d[s, blk * TB:(blk + 1) * TB, :].rearrange("(t p) d -> p t d", p=128)
            for t_ in range(NT):
                mk.dma("sp", oview[:, t_, :], xres[:, t_, :], reads=[("xres", t_)], writes=[("out", s, blk, t_)])
    outkeys = [("out", s, blk, t_) for s in range(nseq) for blk in range(nblk) for t_ in range(NT)]
    mk.final_wait("sp", outkeys + ["tap_" + n for n in tap_d])
    mk.emit()
    mk.close()
    return nc, mk


def host_consts():
    f32 = np.float32
    p = np.arange(128)
    ident = np.eye(128, dtype=f32)
    bones = (p[:, None] // 64 == p[None, :] // 64).astype(f32)
    prot = np.zeros((128, 128), f32)
    for m in range(128):
        if m % 64 < 32:
            prot[m + 32, m] = -1.0
        else:
            prot[m - 32, m] = 1.0
    pos = np.arange(2048, dtype=f32)
    inv = (np.float32(10000.0) ** (-np.arange(32, dtype=f32) / np.float32(32))).astype(f32)
    ang = (pos[:, None] * inv[None, :]).astype(f32)
    cosv = np.cos(ang).astype(f32)
    sinv = np.sin(ang).astype(f32)
    jidx = (p % 64) % 32
    cosT = np.ascontiguousarray(cosv[:, jidx].T)
    sinT = np.ascontiguousarray(sinv[:, jidx].T)
    log_g = np.log(1.0 - np.exp2(-5.0 - np.arange(6, dtype=f32))).astype(f32)
    i64 = np.arange(64, dtype=f32)
    same = (p[:, None] // 64 == p[None, :] // 64)
    dist = np.abs(p[:, None] - p[None, :]).astype(f32)
    dmatT = np.zeros((128, 6, 128), f32)
    for h in range(6):
        dmatT[:, (h % 2) * 3 + h // 2, :] = np.where(same, np.exp(log_g[h] * dist), 0.0)
    kdq = np.zeros((128, 2, 3, 64), f32)
    cdt = np.zeros((128, 3), f32)
    for hp in range(3):
        for hh in range(2):
            h = 2 * hp + hh
            kdq[hh * 64:(hh + 1) * 64, 0, hp, :] = np.exp(log_g[h] * (63.0 - i64))[None, :]
            kdq[hh * 64:(hh + 1) * 64, 1, hp, :] = np.exp(log_g[h] * (i64 + 1.0))[None, :]
            cdt[hh * 64:(hh + 1) * 64, hp] = np.exp(log_g[h] * 64.0)
    su = (same & (p[:, None] < p[None, :])).astype(f32)
    uu = (same & (p[:, None] <= p[None, :])).astype(f32)
    sl = (same & (p[:, None] > p[None, :])).astype(f32)
    msk4 = np.concatenate([su, uu, su, uu], axis=1)
    cmask = np.ones((128, 512), f32)
    cmask[:, ::64] = 0.0
    return dict(ident=ident, bones=bones, prot=prot, cosT=cosT, sinT=sinT, dmatT=dmatT.reshape(128, 768),
                kdq=kdq.reshape(128, 384), cdt=cdt, msk4=msk4, mskL=sl, cmask=cmask)


def host_params(inp):
    f32 = np.float32
    g = lambda k: np.asarray(inp[k], dtype=f32)

    def col3(v):
        return np.ascontiguousarray(v.reshape(3, 128).T)

    pp = np.zeros((128, L, NPP), f32)
    for l in range(L):
        pp[:, l, 0:11] = g("rw_mu")[l].reshape(11, 128).T
        pp[:, l, 11:14] = col3(g("rw_w0")[l])
        pp[:, l, 14:17] = col3(g("rw_a0")[l])
        pp[:, l, 17:20] = col3(g("rw_k_k")[l])
        pp[:, l, 20:23] = col3(g("rw_k_a")[l])
        pp[:, l, 23:26] = col3(g("rw_r_k")[l].reshape(384))
        pp[:, l, 26:29] = col3(g("rw_ln_g")[l])
        pp[:, l, 29:32] = col3(g("rw_ln_b")[l])
        pp[:, l, 32:35] = col3(g("ret_gn_g")[l])
        pp[:, l, 35:38] = col3(g("ret_gn_b")[l])
    convw = np.ascontiguousarray(g("ffn_conv_w").transpose(0, 2, 1).reshape(L, 44, 128, 3).transpose(2, 0, 1, 3)).reshape(128, L, 132)
    convb = np.ascontiguousarray(g("ffn_conv_b").reshape(L, 44, 128).transpose(2, 0, 1))
    lorab = np.ascontiguousarray(np.concatenate([g("rw_w_up"), g("rw_a_up")], axis=1).transpose(1, 0, 2))
    gup = np.ascontiguousarray(g("rw_g_up").transpose(1, 0, 2))
    lng = np.concatenate([g("ln_in_g")[None], g("ln1_g"), g("ln2_g")], axis=0)
    lnb = np.concatenate([g("ln_in_b")[None], g("ln1_b"), g("ln2_b")], axis=0)
    rb = g("attn_rel_bias")
    ext = np.concatenate([rb, np.full((L, 4, 1), -30000.0, f32)], axis=-1)
    kj = np.arange(128)[:, None, None]
    dt = np.arange(5)[None, :, None]
    qi = np.arange(128)[None, None, :]
    rel = 128 * dt + qi - kj
    dch = 2 * dt + qi // 64 - kj // 64
    idx = np.clip(rel, -63, 128) + 63
    idx = np.where((dch >= 0) & (dch <= 8), idx, 192)
    biasT = ext[:, :, idx]
    biasT = np.ascontiguousarray(biasT.transpose(0, 2, 1, 3, 4)).reshape(L, 128, 4 * 5 * 128)
    return dict(w_in=g("w_in"), w_out=g("w_out"), w_up=g("ffn_w_up"), w_down=g("ffn_w_down"), lng=lng, lnb=lnb, pp=pp,
                convw=convw, convb=convb, lorab=lorab, gup=gup, biasT=biasT)


_CACHE = {}


def kernel(**inputs):
    x = np.asarray(inputs["x"], dtype=np.float32)
    Bt, S, _ = x.shape
    ncores = 8
    nseq = Bt // ncores
    nblk = S // TB
    key = (nseq, nblk)
    if key not in _CACHE:
        _CACHE[key] = build(nseq, nblk)[0]
    nc = _CACHE[key]
    shared = dict(host_consts())
    shared.update(host_params(inputs))
    in_maps = []
    for c in range(ncores):
        m = dict(shared)
        m["x"] = np.ascontiguousarray(x[c * nseq:(c + 1) * nseq])
        in_maps.append(m)
    res = run_bass_kernel_spmd(nc, in_maps, core_ids=list(range(ncores)))
    return np.concatenate([r["out"] for r in res.results], axis=0).astype(np.float32)
```

```python
import numpy as np
from contextlib import ExitStack
import concourse.bass as bass
import concourse.mybir as mybir
from concourse.bass_utils import run_bass_kernel_spmd

F32 = mybir.dt.float32
BF16 = mybir.dt.bfloat16
AF = mybir.ActivationFunctionType
ALU = mybir.AluOpType

EPOCH = 30000
NDMASEM = 6

D = 1024
TB = 512
NT = 4
L = 2
DFF = 2816
NJ = 22
OFF_RET = 1408
OFF_ATT = 2944
C0 = float(np.exp(-0.5))
ALPHA = float(4.0 ** 0.25)
LN_EPS = 1e-5
NPP = 44
NSLOT = 98
STAGES = {"att", "ret", "rwkv", "out", "ffn"}
CUT = None


class MK:
    ENGS = ("pe", "act", "dve", "pool", "sp")

    def __init__(self, nc):
        self.nc = nc
        self.es = ExitStack()
        self.ops = {e: [] for e in self.ENGS}
        self.count = {}
        self.sems = {}
        self.known = {e: {} for e in self.ENGS}
        self.lastw = {}
        self.readers = {}
        self.dma_rr = {e: 0 for e in self.ENGS}
        self.muted = False
        for e in self.ENGS:
            self.count[e] = 0
            self.sems[e] = []
        self.dmaprod = {}
        for q in ("sp", "pool", "act"):
            for j in range(NDMASEM):
                p = f"dma_{q}_{j}"
                self.count[p] = 0
                self.sems[p] = []
                self.dmaprod[(q, j)] = p

    def sb(self, name, shape, dtype):
        return self.es.enter_context(self.nc.sbuf_tensor("sb_" + name, list(shape), dtype))

    def ps(self, name, shape, dtype=F32):
        return self.es.enter_context(self.nc.psum_tensor("ps_" + name, list(shape), dtype))

    def _sem(self, prod, tick):
        if prod.startswith("dma_"):
            per = EPOCH // 16
            mult = 16
        else:
            per = EPOCH
            mult = 1
        ep = (tick - 1) // per
        lst = self.sems[prod]
        while len(lst) <= ep:
            lst.append(self.es.enter_context(self.nc.semaphore(f"s_{prod}_{len(lst)}")))
        return lst[ep], ((tick - 1) % per + 1) * mult

    def _deps(self, eng, reads, writes):
        deps = {}
        lastw = self.lastw
        for k in reads:
            pt = lastw.get(k)
            if pt is not None and pt[1] > deps.get(pt[0], 0):
                deps[pt[0]] = pt[1]
        for k in writes:
            pt = lastw.get(k)
            if pt is not None and pt[1] > deps.get(pt[0], 0):
                deps[pt[0]] = pt[1]
            rd = self.readers.get(k)
            if rd:
                for p, t in rd.items():
                    if t > deps.get(p, 0):
                        deps[p] = t
        waits = []
        kn = self.known[eng]
        for p, t in deps.items():
            if p == eng and eng == "pe":
                continue
            if kn.get(p, 0) >= t:
                continue
            kn[p] = t
            waits.append(self._sem(p, t))
        return waits

    def _commit(self, prod, tick, reads, writes):
        for k in reads:
            self.readers.setdefault(k, {})[prod] = tick
        for k in writes:
            self.lastw[k] = (prod, tick)
            self.readers[k] = {}

    def op(self, eng, fn, reads=(), writes=()):
        if self.muted:
            return
        waits = self._deps(eng, reads, writes)
        self.count[eng] += 1
        tick = self.count[eng]
        sem, _ = self._sem(eng, tick)
        self.ops[eng].append((waits, fn, sem, 1))
        self._commit(eng, tick, reads, writes)

    def dma(self, q, out, in_, reads=(), writes=()):
        if self.muted:
            return
        j = self.dma_rr[q]
        self.dma_rr[q] = (j + 1) % NDMASEM
        prod = self.dmaprod[(q, j)]
        waits = self._deps(q, reads, writes)
        prev = self.count[prod]
        if prev > 0 and self.known[q].get(prod, 0) < prev:
            self.known[q][prod] = prev
            waits.append(self._sem(prod, prev))
        self.count[prod] += 1
        tick = self.count[prod]
        sem, _ = self._sem(prod, tick)
        fn = lambda e, out=out, in_=in_: e.dma_start(out=out, in_=in_)
        self.ops[q].append((waits, fn, sem, 16))
        self._commit(prod, tick, reads, writes)

    def final_wait(self, eng, keys):
        waits = self._deps(eng, keys, ())
        self.ops[eng].append((waits, None, None, 0))

    def emit(self):
        block = self.es.enter_context(self.nc.Block())
        ops = self.ops

        def replay(e, lst):
            for waits, fn, sem, inc in lst:
                for (s, v) in waits:
                    e.wait_ge(s, v)
                if fn is not None:
                    fn(e).then_inc(sem, inc)

        @block.tensor
        def _(e):
            replay(e, ops["pe"])

        @block.scalar
        def _(e):
            replay(e, ops["act"])

        @block.vector
        def _(e):
            replay(e, ops["dve"])

        @block.gpsimd
        def _(e):
            replay(e, ops["pool"])

        @block.sync
        def _(e):
            replay(e, ops["sp"])

    def close(self):
        self.es.close()


class Buf:
    def __init__(self, ap, keys):
        self.ap = ap
        self.keys = keys

    def v(self, pat, **kw):
        return self.ap.rearrange(pat, **kw)


def build(nseq, nblk, taps=None):
    nc = bass.Bass("TRN2", target_bir_lowering=False)
    mk = MK(nc)
    S = nblk * TB

    def din(name, shape):
        return nc.dram_tensor(name, list(shape), F32, kind="ExternalInput").ap()

    x_d = din("x", [nseq, S, D])
    w_in_d = din("w_in", [L, D, 3712])
    w_out_d = din("w_out", [L, D, D])
    w_up_d = din("w_up", [L, D, 2 * DFF])
    w_dn_d = din("w_down", [L, DFF, D])
    lng_d = din("lng", [1 + 2 * L, D])
    lnb_d = din("lnb", [1 + 2 * L, D])
    pp_d = din("pp", [128, L, NPP])
    cw_d = din("convw", [128, L, 44 * 3])
    cb_d = din("convb", [128, L, 44])
    lorab_d = din("lorab", [128, L, 384])
    gup_d = din("gup", [128, L, 384])
    bias_d = din("biasT", [L, 128, 4 * 5 * 128])
    ident_d = din("ident", [128, 128])
    bones_d = din("bones", [128, 128])
    prot_d = din("prot", [128, 128])
    cos_d = din("cosT", [128, 2048])
    sin_d = din("sinT", [128, 2048])
    dmat_d = din("dmatT", [128, 768])
    kdq_d = din("kdq", [128, 384])
    cd_d = din("cdt", [128, 3])
    msk1_d = din("msk4", [128, 512])
    mskL_d = din("mskL", [128, 128])
    cmask_d = din("cmask", [128, 512])
    out_d = nc.dram_tensor("out", [nseq, S, D], F32, kind="ExternalOutput").ap()
    tap_d = {}
    if taps:
        for nm, shp in taps.items():
            tap_d[nm] = nc.dram_tensor("tap_" + nm, list(shp), F32, kind="ExternalOutput").ap()

    ident32 = mk.sb("ident32", [128, 128], F32)
    identbf = mk.sb("identbf", [128, 128], BF16)
    bonesbf = mk.sb("bonesbf", [128, 128], BF16)
    bones32 = mk.sb("bones32", [128, 128], F32)
    prot32 = mk.sb("prot32", [128, 128], F32)
    dmatT = mk.sb("dmatT", [128, 768], F32)
    kdq = mk.sb("kdq", [128, 2, 3, 64], F32)
    cdt = mk.sb("cdt", [128, 3, 1], F32)
    msk4 = mk.sb("msk4", [128, 512], F32)
    mskL = mk.sb("mskL", [128, 1, 128], F32)
    cmask = mk.sb("cmask", [128, 512], F32)
    pp = mk.sb("pp", [128, L, NPP], F32)
    cw = mk.sb("cw", [128, L, 44, 3], F32)
    cb = mk.sb("cb", [128, L, 44], F32)
    lorab = mk.sb("lorab", [128, L, 384], BF16)
    gup = mk.sb("gup", [128, L, 384], BF16)
    epst = mk.sb("epst", [128, 4], F32)
    xres = mk.sb("xres", [128, NT, D], F32)
    xT = mk.sb("xT", [128, 8, TB], BF16)
    yT = mk.sb("yT", [128, 8, TB], BF16)
    s32 = [mk.sb(f"s32_{i}", [128, D], F32) for i in range(2)]
    lng_t = mk.sb("lng_t", [128, D], F32)
    lnb_t = mk.sb("lnb_t", [128, D], F32)
    W = [mk.sb(f"W{i}", [128, 8, 512], BF16) for i in range(3)]
    kTa = [mk.sb(f"kTa{l}", [128, 2, 1024], BF16) for l in range(L)]
    vA = [mk.sb(f"vA{l}", [128, 8, 4, 65], BF16) for l in range(L)]
    R32 = [mk.sb(f"R32_{l}", [128, 3, 64], F32) for l in range(L)]
    S32 = [mk.sb(f"S32_{l}", [128, 3, 64], F32) for l in range(L)]
    hal = [mk.sb(f"hal{l}", [128, 11], F32) for l in range(L)]
    chal = [mk.sb(f"chal{l}", [128, 44, 2], F32) for l in range(L)]
    lnst = [(mk.sb(f"st{i}", [128, 2, 6], F32), mk.sb(f"mv{i}", [128, 2], F32), mk.sb(f"sd{i}", [128, 1], F32),
             mk.sb(f"rstd{i}", [128, 1], F32), mk.sb(f"nmr{i}", [128, 1], F32)) for i in range(2)]
    pcs = mk.sb("pcs", [128, 3, 8], F32)
    rc = mk.sb("rc", [128, 4, 1], F32)
    A = mk.sb("A", [128, NSLOT, 512], BF16)
    B = [mk.ps(f"B{i}", [128, 512], F32) for i in range(8)]
    Bk = [f"B{i}" for i in range(8)]

    class Arena:
        def __init__(self):
            self.top = 0

        def reset(self):
            self.top = 0

        def bf(self, n):
            s0 = self.top
            self.top += n
            assert self.top <= NSLOT, self.top
            ap = A[:, s0:s0 + n, :].rearrange("p a b -> p (a b)")
            return Buf(ap, [("A", s) for s in range(s0, s0 + n)])

        def f32(self, n):
            b = self.bf(2 * n)
            return Buf(b.ap.bitcast(F32), b.keys)

    ar = Arena()

    def mm(out, lhsT, rhs, start, stop, r, w):
        mk.op("pe", lambda e: e.matmul(out, lhsT=lhsT, rhs=rhs, start=start, stop=stop), reads=r, writes=w)

    def tr(out, in_, idn, r, w):
        mk.op("pe", lambda e: e.transpose(out, in_, idn), reads=r, writes=w)

    def act(out, in_, func, r, w, bias=None, scale=1.0):
        if bias is None:
            mk.op("act", lambda e: e.activation(out=out, in_=in_, func=func, scale=scale), reads=r, writes=w)
        else:
            mk.op("act", lambda e: e.activation(out=out, in_=in_, func=func, bias=bias, scale=scale), reads=r, writes=w)

    def tt(eng, out, in0, in1, op, r, w):
        mk.op(eng, lambda e: e.tensor_tensor(out=out, in0=in0, in1=in1, op=op), reads=r, writes=w)

    def ts(eng, out, in0, s1, s2, op0, op1, r, w):
        if op1 is None:
            mk.op(eng, lambda e: e.tensor_scalar(out=out, in0=in0, scalar1=s1, scalar2=None, op0=op0), reads=r, writes=w)
        else:
            mk.op(eng, lambda e: e.tensor_scalar(out=out, in0=in0, scalar1=s1, scalar2=s2, op0=op0, op1=op1), reads=r, writes=w)

    def stt(out, in0, scalar, in1, op0, op1, r, w):
        mk.op("dve", lambda e: e.scalar_tensor_tensor(out=out, in0=in0, scalar=scalar, in1=in1, op0=op0, op1=op1),
              reads=r, writes=w)

    def cp(eng, out, in_, r, w):
        if eng == "act":
            mk.op("act", lambda e: e.activation(out=out, in_=in_, func=AF.Copy), reads=r, writes=w)
        else:
            mk.op(eng, lambda e: e.tensor_copy(out=out, in_=in_), reads=r, writes=w)

    def memset(eng, ap, val, w):
        mk.op(eng, lambda e: e.memset(ap, val), writes=w)

    def tap(name, ap, keys):
        if name in tap_d:
            n = ap.shape[1]
            for c0 in range(0, n, 512):
                mk.dma("pool", tap_d[name][:, c0:c0 + 512], ap[:, c0:c0 + 512], reads=keys, writes=["tap_" + name])

    def run_interleaved(gens):
        gens = list(gens)
        while gens:
            for g_ in list(gens):
                try:
                    next(g_)
                except StopIteration:
                    gens.remove(g_)

    wrr = [0]

    def wload(src, ncols, nk=8):
        i = wrr[0] % 3
        wrr[0] += 1
        dst = W[i][:, 0:nk, 0:ncols]
        mk.dma("pool", dst, src.rearrange("(kc p) n -> p kc n", p=128), writes=[f"W{i}"])
        return W[i], f"W{i}"

    mk.dma("sp", ident32[:], ident_d, writes=["ident32"])
    mk.dma("pool", identbf[:], ident_d, writes=["identbf"])
    mk.dma("pool", bonesbf[:], bones_d, writes=["bonesbf"])
    mk.dma("sp", bones32[:], bones_d, writes=["bones32"])
    mk.dma("sp", prot32[:], prot_d, writes=["prot32"])
    mk.dma("sp", dmatT[:], dmat_d, writes=["dmatT"])
    mk.dma("sp", kdq[:].rearrange("p a b c -> p (a b c)"), kdq_d, writes=["kdq"])
    mk.dma("sp", cdt[:].rearrange("p a b -> p (a b)"), cd_d, writes=["cdt"])
    mk.dma("sp", msk4[:], msk1_d, writes=["msk4"])
    mk.dma("sp", mskL[:].rearrange("p a b -> p (a b)"), mskL_d, writes=["mskL"])
    mk.dma("sp", cmask[:], cmask_d, writes=["cmask"])
    mk.dma("sp", pp[:], pp_d, writes=["pp"])
    mk.dma("sp", cw[:].rearrange("p l a b -> p l (a b)"), cw_d, writes=["cw"])
    mk.dma("sp", cb[:], cb_d, writes=["cb"])
    mk.dma("pool", lorab[:], lorab_d, writes=["lorab"])
    mk.dma("pool", gup[:], gup_d, writes=["gup"])
    ts("dve", bones32[:], bones32[:], 1.0 / 64.0, None, ALU.mult, None, ["bones32"], ["bones32"])
    memset("dve", epst[:, 0:1], LN_EPS, ["epst"])
    memset("dve", epst[:, 1:2], 64e-5, ["epst"])
    memset("dve", epst[:, 2:3], 1e-5, ["epst"])
    for l in range(L):
        pass
    omm = mk.sb("omm", [128, L, 11], F32)
    omka = mk.sb("omka", [128, L, 3], F32)
    ts("dve", omm[:], pp[:, :, 0:11], -1.0, 1.0, ALU.mult, ALU.add, ["pp"], ["omm"])
    ts("dve", omka[:], pp[:, :, 20:23], -1.0, 1.0, ALU.mult, ALU.add, ["pp"], ["omka"])
    for l in range(L):
        memset("dve", vA[l][:], 1.0, [("vA", l, i) for i in range(8)])

    PPk = ["pp", "omm", "omka"]

    def ln_a(t_, s_ap, s_key):
        i = t_ % 2
        st_, mv_, sd_, rs_, nm_ = lnst[i]
        k_ = f"lnst{i}"
        mk.op("dve", lambda e: e.bn_stats(out=st_[:, 0, :], in_=s_ap[:, 0:512]), reads=[s_key], writes=[k_])
        mk.op("dve", lambda e: e.bn_stats(out=st_[:, 1, :], in_=s_ap[:, 512:1024]), reads=[s_key], writes=[k_])
        mk.op("dve", lambda e: e.bn_aggr(out=mv_[:], in_=st_[:].rearrange("p a b -> p (a b)")), reads=[k_], writes=[k_])
        act(sd_[:], mv_[:, 1:2], AF.Sqrt, [k_, "epst"], [k_], bias=epst[:, 0:1])
        mk.op("dve", lambda e: e.reciprocal(out=rs_[:], in_=sd_[:]), reads=[k_], writes=[k_])
        stt(nm_[:], mv_[:, 0:1], -1.0, rs_[:], ALU.mult, ALU.mult, [k_], [k_])
        xk = ("xres", t_)
        act(xres[:, t_, :], s_ap, AF.Identity, [s_key, k_], [xk], bias=nm_[:], scale=rs_[:])
        tt("dve", xres[:, t_, :], xres[:, t_, :], lng_t[:], ALU.mult, [xk, "lng_t"], [xk])
        tt("dve", xres[:, t_, :], xres[:, t_, :], lnb_t[:], ALU.add, [xk, "lnb_t"], [xk])

    def ln_t(t_, pbank):
        xk = ("xres", t_)
        for hb in range(2):
            bk = pbank + hb
            for q in range(4):
                kc = hb * 4 + q
                tr(B[bk][:, q * 128:(q + 1) * 128], xres[:, t_, kc * 128:(kc + 1) * 128], ident32[:],
                   [xk, "ident32"], [Bk[bk]])
            cp("act" if hb == 0 else "dve", xT[:, hb * 4:hb * 4 + 4, t_ * 128:(t_ + 1) * 128],
               B[bk][:].rearrange("p (a b) -> p a b", b=128), [Bk[bk]], [("xT", t_)])

    def ln_tile(t_, s_ap, s_key, pbank):
        ln_a(t_, s_ap, s_key)
        ln_t(t_, pbank)

    XTK = [("xT", t_) for t_ in range(NT)]
    YTK = [("yT", c) for c in range(8)]

    def load_ln(idx):
        mk.dma("sp", lng_t[:], lng_d[idx].partition_broadcast(128), writes=["lng_t"])
        mk.dma("sp", lnb_t[:], lnb_d[idx].partition_broadcast(128), writes=["lnb_t"])

    def headnorm3(ys, epscol, gcol, bcol, scr, l):
        R3 = range(3)
        for i in R3:
            mm(B[i][:], bones32[:], ys[i].ap, True, True, ys[i].keys + ["bones32"], [Bk[i]])
        for i in R3:
            tt("dve", ys[i].ap, ys[i].ap, B[i][:], ALU.subtract, ys[i].keys + [Bk[i]], ys[i].keys)
        for i in R3:
            act(scr[i].ap, ys[i].ap, AF.Square, ys[i].keys, scr[i].keys)
        for i in R3:
            mm(B[3 + i][:], bones32[:], scr[i].ap, True, True, scr[i].keys + ["bones32"], [Bk[3 + i]])
        for i in R3:
            act(scr[i].ap, B[3 + i][:], AF.Sqrt, [Bk[3 + i], "epst"], scr[i].keys, bias=epst[:, epscol:epscol + 1])
        for i in R3:
            mk.op("dve", lambda e, sc=scr[i]: e.reciprocal(out=sc.ap, in_=sc.ap), reads=scr[i].keys, writes=scr[i].keys)
        for i in R3:
            tt("dve", ys[i].ap, ys[i].ap, scr[i].ap, ALU.mult, ys[i].keys + scr[i].keys, ys[i].keys)
        for i in R3:
            ts("dve", ys[i].ap, ys[i].ap, pp[:, l, gcol + i:gcol + i + 1], pp[:, l, bcol + i:bcol + i + 1], ALU.mult, ALU.add,
               ys[i].keys + PPk, ys[i].keys)

    for s in range(nseq):
        for l in range(L):
            memset("dve", R32[l][:], 0.0, [f"R32_{l}"])
            memset("dve", S32[l][:], 0.0, [f"S32_{l}"])
            memset("dve", hal[l][:], 0.0, [f"hal{l}"])
            memset("dve", chal[l][:], 0.0, [f"chal{l}"])
        for blk in range(nblk):
            load_ln(0)
            xin = x_d[s, blk * TB:(blk + 1) * TB, :].rearrange("(t p) d -> p t d", p=128)
            for t_ in range(NT):
                sb_ = s32[t_ % 2]
                sk = f"s32_{t_ % 2}"
                mk.dma("sp", sb_[:], xin[:, t_, :], writes=[sk])
                ln_a(t_, sb_[:], sk)
                if t_ >= 1:
                    ln_t(t_ - 1, 4 + 2 * ((t_ - 1) % 2))
            ln_t(NT - 1, 4 + 2 * ((NT - 1) % 2))
            for l in range(L):
                P = lambda c0, c1=None: pp[:, l, c0:(c0 + 1 if c1 is None else c1)]
                if "att" in STAGES:
                  ar.reset()
                  biasb = ar.bf(5)
                  qTa = ar.bf(2)
                  pTs = [ar.bf(1) for _ in range(3)]
                  yatt = ar.bf(1)
                  for hh_ in range(2):
                      mk.dma("pool", biasb.ap[:, hh_ * 1280:(hh_ + 1) * 1280], bias_d[l][:, hh_ * 1280:(hh_ + 1) * 1280],
                             writes=biasb.keys)
                  Wqk, Wqk_k = wload(w_in_d[l][:, OFF_ATT:OFF_ATT + 512], 512)
                  Wv_, Wv_k = wload(w_in_d[l][:, OFF_ATT + 512:OFF_ATT + 768], 256)
                  slot = blk % 2
                  for hp in range(2):
                      for kc in range(8):
                          mm(B[hp][:], Wqk[:, kc, hp * 128:(hp + 1) * 128], xT[:, kc, :], kc == 0, kc == 7,
                             [Wqk_k] + XTK, [Bk[hp]])
                      act(qTa.ap[:, hp * 512:(hp + 1) * 512], B[hp][:], AF.Copy, [Bk[hp]], qTa.keys, scale=0.125)
                  for hp in range(2):
                      for kc in range(8):
                          mm(B[2 + hp][:], Wqk[:, kc, 256 + hp * 128:256 + (hp + 1) * 128], xT[:, kc, :], kc == 0, kc == 7,
                             [Wqk_k] + XTK, [Bk[2 + hp]])
                      cp("dve", kTa[l][:, hp, slot * 512:(slot + 1) * 512], B[2 + hp][:], [Bk[2 + hp]],
                         [("kTa", l, slot * 4 + i) for i in range(4)])
                  for t_ in range(NT):
                      tg = (blk * 4 + t_) % 8
                      bk = 2 + t_ % 2
                      for kc in range(8):
                          mm(B[bk][:, 0:256], xT[:, kc, t_ * 128:(t_ + 1) * 128], Wv_[:, kc, 0:256], kc == 0, kc == 7,
                             [Wv_k, ("xT", t_)], [Bk[bk]])
                      cp("act", vA[l][:, tg, :, 0:64], B[bk][:, 0:256].rearrange("p (h d) -> p h d", d=64), [Bk[bk]],
                         [("vA", l, tg)])
                  biasv = biasb.v("p (h t q) -> p h t q", h=4, t=5)
                  qv = qTa.v("p (a t) -> p a t", a=2)
                  its = []
                  for t_ in range(NT):
                      tq = blk * 4 + t_
                      for h in range(4):
                          kts = list(range(max(0, tq - 4), tq + 1))
                          for kt in kts:
                              its.append((t_, tq, h, kt, kt == kts[0], kt == kts[-1], h == 3 and kt == kts[-1]))

                  def att_score(i):
                      t_, tq, h, kt, first, last, fin = its[i]
                      hp, po = h // 2, (h % 2) * 64
                      sbk = 4 + (h % 2) + 2 * (i % 2)
                      kr = kt % 8
                      mm(B[sbk][:, 0:128], kTa[l][po:po + 64, hp, kr * 128:(kr + 1) * 128],
                         qv[po:po + 64, hp, t_ * 128:(t_ + 1) * 128], True, False,
                         [("kTa", l, kr)] + qTa.keys, [Bk[sbk]])
                      mm(B[sbk][:, 0:128], identbf[:], biasv[:, h, tq - kt, :], False, True,
                         ["identbf"] + biasb.keys, [Bk[sbk]])

                  def att_pv(i):
                      t_, tq, h, kt, first, last, fin = its[i]
                      sbk = 4 + (h % 2) + 2 * (i % 2)
                      kr = kt % 8
                      pT = pTs[i % 3]
                      POv = B[t_ % 2][:, 0:260].rearrange("p (h d) -> p h d", d=65)
                      POk = Bk[t_ % 2]
                      act(pT.ap[:, 0:128], B[sbk][:, 0:128], AF.Exp, [Bk[sbk]], pT.keys)
                      mm(POv[:, h, :], pT.ap[:, 0:128], vA[l][:, kr, h, :], first, last, pT.keys + [("vA", l, kr)], [POk])
                      if fin:
                          mk.op("dve", lambda e, POv=POv: e.reciprocal(out=rc[:], in_=POv[:, :, 64:65]), reads=[POk], writes=["rc"])
                          tt("dve", yatt.ap[:, 0:256].rearrange("p (h d) -> p h d", d=64), POv[:, :, 0:64],
                             rc[:].to_broadcast([128, 4, 64]), ALU.mult, [POk, "rc"], yatt.keys)
                          tb = 2 + t_ % 2
                          Bb = B[tb][:].bitcast(BF16)
                          for hp in range(2):
                              tr(Bb[:, hp * 128:(hp + 1) * 128], yatt.ap[:, hp * 128:(hp + 1) * 128], identbf[:],
                                 yatt.keys + ["identbf"], [Bk[tb]])
                          cp("act", yT[:, 6:8, t_ * 128:(t_ + 1) * 128], Bb[:, 0:256].rearrange("p (a b) -> p a b", b=128),
                             [Bk[tb]], [("yT", 6), ("yT", 7)])

                  att_score(0)
                  for i in range(len(its)):
                      if i + 1 < len(its):
                          att_score(i + 1)
                      att_pv(i)

                if "ret" in STAGES:
                  ar.reset()
                  cosb = ar.f32(1)
                  sinb = ar.f32(1)
                  q32s = [ar.f32(1) for _ in range(2)]
                  tA = [ar.f32(1) for _ in range(2)]
                  tBm = [ar.f32(1) for _ in range(2)]
                  qrT = [ar.bf(1) for _ in range(3)]
                  krT = [ar.bf(1) for _ in range(3)]
                  qdT = [ar.bf(1) for _ in range(3)]
                  kdT = [ar.bf(1) for _ in range(3)]
                  sg = [ar.bf(1) for _ in range(3)]
                  kdtok = ar.bf(3)
                  v_r = ar.bf(3)
                  scms = [ar.bf(2) for _ in range(4)]
                  Rsnap = ar.bf(3)
                  yr = [ar.f32(1) for _ in range(3)]
                  hsc = [ar.f32(1) for _ in range(3)]
                  mk.dma("sp", cosb.ap, cos_d[:, blk * TB:(blk + 1) * TB], writes=cosb.keys)
                  mk.dma("sp", sinb.ap, sin_d[:, blk * TB:(blk + 1) * TB], writes=sinb.keys)
                  Wq, Wq_k = wload(w_in_d[l][:, OFF_RET:OFF_RET + 384], 384)
                  Wk, Wk_k = wload(w_in_d[l][:, OFF_RET + 384:OFF_RET + 768], 384)
                  Wg, Wg_k = wload(w_in_d[l][:, OFF_RET + 1152:OFF_RET + 1536], 384)
                  qq = 0
                  for hp in range(3):
                      for (Wx, Wx_k, scl, dst, decidx, dst2) in ((Wq, Wq_k, 1.0, qrT[hp], 1, qdT[hp]),
                                                                 (Wk, Wk_k, 0.125, krT[hp], 0, kdT[hp])):
                          i2 = qq % 2
                          qq += 1
                          pz = i2 * 2
                          for kc in range(8):
                              mm(B[pz][:], Wx[:, kc, hp * 128:(hp + 1) * 128], xT[:, kc, :], kc == 0, kc == 7,
                                 [Wx_k] + XTK, [Bk[pz]])
                          act(q32s[i2].ap, B[pz][:], AF.Copy, [Bk[pz]], q32s[i2].keys, scale=scl)
                          mm(B[pz + 1][:], prot32[:], q32s[i2].ap, True, True, q32s[i2].keys + ["prot32"], [Bk[pz + 1]])
                          tt("dve", tA[i2].ap, q32s[i2].ap, cosb.ap, ALU.mult, q32s[i2].keys + cosb.keys, tA[i2].keys)
                          tt("dve", tBm[i2].ap, B[pz + 1][:], sinb.ap, ALU.mult, [Bk[pz + 1]] + sinb.keys, tBm[i2].keys)
                          tt("dve", tA[i2].ap, tA[i2].ap, tBm[i2].ap, ALU.add, tA[i2].keys + tBm[i2].keys, tA[i2].keys)
                          cp("act", dst.ap, tA[i2].ap, tA[i2].keys, dst.keys)
                          tt("dve", dst2.v("p (c j) -> p c j", j=64), tA[i2].v("p (c j) -> p c j", j=64),
                             kdq[:, decidx, hp:hp + 1, :].to_broadcast([128, 8, 64]), ALU.mult,
                             tA[i2].keys + ["kdq"], dst2.keys)
                      for kc in range(8):
                          mm(B[4][:], Wg[:, kc, hp * 128:(hp + 1) * 128], xT[:, kc, :], kc == 0, kc == 7,
                             [Wg_k] + XTK, [Bk[4]])
                      act(sg[hp].ap, B[4][:], AF.Silu, [Bk[4]], sg[hp].keys)
                  mk.muted = mk.muted or CUT == "ret1"
                  Wv_, Wv_k = wload(w_in_d[l][:, OFF_RET + 768:OFF_RET + 1152], 384)
                  vrv = v_r.v("p (t c) -> p t c", t=4)
                  kdv = kdtok.v("p (t c) -> p t c", t=4)
                  for t_ in range(NT):
                      bk = 5 + t_ % 2
                      for kc in range(8):
                          mm(B[bk][:, 0:384], xT[:, kc, t_ * 128:(t_ + 1) * 128], Wv_[:, kc, 0:384], kc == 0, kc == 7,
                             [Wv_k, ("xT", t_)], [Bk[bk]])
                      cp("dve", vrv[:, t_, :], B[bk][:, 0:384], [Bk[bk]], v_r.keys)
                      Bb = B[7][:].bitcast(BF16)
                      for hp in range(3):
                          tr(Bb[:, hp * 128:(hp + 1) * 128], kdT[hp].ap[:, t_ * 128:(t_ + 1) * 128], identbf[:],
                             kdT[hp].keys + ["identbf"], [Bk[7]])
                      cp("act", kdv[:, t_, :], Bb[:, 0:384], [Bk[7]], kdtok.keys)
                  mk.muted = mk.muted or CUT == "ret2"
                  rsv = Rsnap.v("p (c x) -> p c x", c=8)
                  Rk = f"R32_{l}"

                  def gen_p1():
                      for c in range(8):
                          t_, tp = c // 2, (c % 2) * 64
                          bu = c % 2
                          for h in range(6):
                              hp, po = h // 2, (h % 2) * 64
                              mm(B[bu][po:po + 64, hp * 64:(hp + 1) * 64], kdv[tp:tp + 64, t_, h * 64:(h + 1) * 64],
                                 vrv[tp:tp + 64, t_, h * 64:(h + 1) * 64], True, True, kdtok.keys + v_r.keys, [Bk[bu]])
                          yield
                          cp("act", rsv[:, c, :], R32[l][:].rearrange("p a b -> p (a b)"), [Rk], Rsnap.keys)
                          tt("dve", R32[l][:], R32[l][:], cdt[:].to_broadcast([128, 3, 64]), ALU.mult, [Rk, "cdt"], [Rk])
                          tt("dve", R32[l][:], R32[l][:], B[bu][:, 0:192].rearrange("p (a b) -> p a b", b=64), ALU.add,
                             [Rk, Bk[bu]], [Rk])
                          yield

                  def gen_sc():
                      for t_ in range(NT):
                          cs_ = slice(t_ * 128, (t_ + 1) * 128)
                          for hh in range(2):
                              po = hh * 64
                              for hp in range(3):
                                  mm(B[2 + hh][:, hp * 128:(hp + 1) * 128], krT[hp].ap[po:po + 64, cs_], qrT[hp].ap[po:po + 64, cs_],
                                     True, True, krT[hp].keys + qrT[hp].keys, [Bk[2 + hh]])
                          yield
                          for hh in range(2):
                              tt("dve", scms[t_].ap[:, hh * 384:(hh + 1) * 384], B[2 + hh][:, 0:384], dmatT[:, hh * 384:(hh + 1) * 384],
                                 ALU.mult, [Bk[2 + hh], "dmatT"], scms[t_].keys)
                          yield

                  run_interleaved([gen_p1(), gen_sc()])
                  mk.muted = mk.muted or CUT == "ret3"
                  for t_ in range(NT):
                      cs_ = slice(t_ * 128, (t_ + 1) * 128)
                      scm = scms[t_]
                      scv = scm.v("p (h i) -> p h i", h=8)
                      pyb = 4 + 2 * (t_ % 2)
                      for hh in range(2):
                          po = hh * 64
                          py = pyb + hh
                          for hp in range(3):
                              c0_ = hp * 128
                              h = 2 * hp + hh
                              mm(B[py][po:po + 64, c0_:c0_ + 128], vrv[:, t_, h * 64:(h + 1) * 64], scv[:, hh * 3 + hp, :], True, False,
                                 v_r.keys + scm.keys, [Bk[py]])
                              for hf in range(2):
                                  c = 2 * t_ + hf
                                  mm(B[py][po:po + 64, c0_ + hf * 64:c0_ + (hf + 1) * 64],
                                     rsv[po:po + 64, c, hp * 64:(hp + 1) * 64],
                                     qdT[hp].ap[po:po + 64, t_ * 128 + hf * 64:t_ * 128 + (hf + 1) * 64], False, hf == 1,
                                     Rsnap.keys + qdT[hp].keys, [Bk[py]])
                      for hh in range(2):
                          po = hh * 64
                          py = pyb + hh
                          for hp in range(3):
                              cp("act", yr[hp].ap[po:po + 64, cs_], B[py][po:po + 64, hp * 128:(hp + 1) * 128], [Bk[py]], yr[hp].keys)
                  mk.muted = mk.muted or CUT == "ret4"
                  headnorm3(yr, 2, 32, 35, hsc, l)
                  for hp in range(3):
                      tt("dve", yT[:, 3 + hp, :], yr[hp].ap, sg[hp].ap, ALU.mult, yr[hp].keys + sg[hp].keys,
                         [("yT", 3 + hp)])

                mk.muted = False
                if "rwkv" in STAGES:
                  ar.reset()
                  z32 = []
                  for _ in range(2):
                      zb_ = ar.bf(3)
                      z32.append(Buf(zb_.ap.bitcast(F32), zb_.keys))
                  t1 = [ar.f32(1) for _ in range(2)]
                  f = [ar.f32(1) for _ in range(10)]
                  f2 = [ar.f32(1) for _ in range(5)]
                  lin = ar.bf(1)
                  sgl = ar.bf(1)
                  sqb = ar.bf(1)
                  rkb = ar.bf(1)
                  bhb = ar.bf(1)
                  khb = ar.bf(1)
                  vbb = ar.bf(1)
                  artT = [ar.bf(2) for _ in range(3)]
                  btT = [ar.bf(1) for _ in range(3)]
                  ktT = [ar.bf(1) for _ in range(3)]
                  bonus = [ar.bf(1) for _ in range(3)]
                  gT = [ar.bf(1) for _ in range(3)]
                  tokm = ar.bf(9)
                  LM = [ar.bf(2) for _ in range(3)]
                  LT = [ar.bf(1) for _ in range(3)]
                  Nm = [ar.bf(1) for _ in range(3)]
                  XX = [[ar.bf(1) for _ in range(2)] for _ in range(3)]
                  rhs_sb = ar.bf(1)
                  sa_sb = ar.bf(1)
                  Sbf = [ar.bf(1) for _ in range(2)]
                  halk = f"hal{l}"
                  sh = [0]

                  def shift(pbk, i, out_ap, out_keys):
                      j = sh[0] % 2
                      sh[0] += 1
                      z = z32[j]
                      cp("act", z.ap[:, 1:513], B[pbk][:], [Bk[pbk]], z.keys)
                      cp("act", z.ap[:, 0:1], hal[l][:, i:i + 1], [halk], z.keys)
                      act(t1[j].ap, B[pbk][:], AF.Identity, [Bk[pbk]] + PPk, t1[j].keys, scale=omm[:, l, i:i + 1])
                      stt(out_ap, z.ap[:, 0:512], P(i), t1[j].ap, ALU.mult, ALU.add, z.keys + t1[j].keys + PPk, out_keys)
                      cp("act", hal[l][:, i:i + 1], z.ap[:, 512:513], z.keys, [halk])

                  Wl, Wl_k = wload(w_in_d[l][:, 1152:1408], 256)
                  Wr, Wr_k = wload(w_in_d[l][:, 0:384], 384)
                  Wk, Wk_k = wload(w_in_d[l][:, 384:768], 384)
                  zs = f[9]
                  for kc in range(8):
                      mm(B[0][:], Wl[:, kc, 0:128], xT[:, kc, :], kc == 0, kc == 7, [Wl_k] + XTK, [Bk[0]])
                  shift(0, 9, zs.ap, zs.keys)
                  act(lin.ap[0:64, :], zs.ap[0:64, :], AF.Tanh, zs.keys, lin.keys)
                  cp("dve", lin.ap[64:128, :], zs.ap[64:128, :], zs.keys, lin.keys)
                  for kc in range(8):
                      mm(B[1][:], Wl[:, kc, 128:256], xT[:, kc, :], kc == 0, kc == 7, [Wl_k] + XTK, [Bk[1]])
                  shift(1, 10, zs.ap, zs.keys)
                  act(sgl.ap, zs.ap, AF.Sigmoid, zs.keys, sgl.keys)
                  Wv_, Wv_k = None, None
                  tokv = tokm.v("p (t q c) -> p t q c", t=4, q=3)
                  Wv_, Wv_k = wload(w_in_d[l][:, 768:1152], 384)

                  def proj_stage(hp_):
                      r_, k_, v_ = (f2[2], f2[3], f2[4]) if hp_ % 2 == 1 else (f[6], f[7], f[8])
                      hs_ = slice(hp_ * 128, (hp_ + 1) * 128)
                      for (Wx, Wx_k, pbk, i, dst) in ((Wr, Wr_k, 5, hp_, r_), (Wk, Wk_k, 6, 3 + hp_, k_), (Wv_, Wv_k, 7, 6 + hp_, v_)):
                          for kc in range(8):
                              mm(B[pbk][:], Wx[:, kc, hs_], xT[:, kc, :], kc == 0, kc == 7, [Wx_k] + XTK, [Bk[pbk]])
                          shift(pbk, i, dst.ap, dst.keys)

                  kkn2 = ar.f32(1)

                  def bufs(hp_):
                      lw, a32, E, Einv, Epv, Dd, r32, k32, v32, kkn = f[0:10]
                      if hp_ % 2 == 1:
                          lw, a32, r32, k32, v32 = f2
                          kkn = kkn2
                      return lw, a32, E, Einv, Epv, Dd, r32, k32, v32, kkn

                  def s2a(hp):
                      lw, a32, E, Einv, Epv, Dd, r32, k32, v32, kkn = bufs(hp)
                      hs = slice(hp * 128, (hp + 1) * 128)
                      mm(B[2][:], lorab[0:64, l, hs], lin.ap[0:64, :], True, True, ["lorab"] + lin.keys, [Bk[2]])
                      act(lw.ap, B[2][:], AF.Sigmoid, [Bk[2]] + PPk, lw.keys, bias=P(11 + hp))
                      mm(B[3][:], lorab[64:128, l, hs], lin.ap[64:128, :], True, True, ["lorab"] + lin.keys, [Bk[3]])
                      act(a32.ap, B[3][:], AF.Sigmoid, [Bk[3]] + PPk, a32.keys, bias=P(14 + hp))
                      mm(B[4][:], gup[:, l, hs], sgl.ap, True, True, ["gup"] + sgl.keys, [Bk[4]])
                      cp("act", gT[hp].ap, B[4][:], [Bk[4]], gT[hp].keys)
                      act(sqb.ap, k32.ap, AF.Square, k32.keys + PPk, sqb.keys, scale=P(17 + hp))
                      mm(B[2][:], bonesbf[:], sqb.ap, True, True, sqb.keys + ["bonesbf"], [Bk[2]])
                      act(kkn.ap, B[2][:], AF.Sqrt, [Bk[2]], kkn.keys)

                  def s2b(hp):
                      lw, a32, E, Einv, Epv, Dd, r32, k32, v32, kkn = bufs(hp)
                      mk.op("dve", lambda e, E=E, lw=lw: e.tensor_tensor_scan(out=E.ap, data0=cmask[:], data1=lw.ap, initial=0.0,
                                                                             op0=ALU.mult, op1=ALU.add),
                            reads=["cmask"] + lw.keys, writes=E.keys)
                      tt("dve", Epv.ap, E.ap, lw.ap, ALU.subtract, E.keys + lw.keys, Epv.keys)
                      csv = E.v("p (c j) -> p c j", j=64)
                      tt("dve", Dd.v("p (c j) -> p c j", j=64), csv[:, :, 63:64].to_broadcast([128, 8, 64]), csv, ALU.subtract,
                         E.keys, Dd.keys)
                      act(Einv.ap, E.ap, AF.Exp, E.keys, Einv.keys, scale=C0)
                      act(E.ap, E.ap, AF.Exp, E.keys, E.keys, scale=-C0)
                      act(Epv.ap, Epv.ap, AF.Exp, Epv.keys, Epv.keys, scale=-C0)
                      act(Dd.ap, Dd.ap, AF.Exp, Dd.keys, Dd.keys, scale=-C0)
                      ts("dve", kkn.ap, kkn.ap, 1e-12, None, ALU.max, None, kkn.keys, kkn.keys)
                      mk.op("dve", lambda e, kkn=kkn: e.reciprocal(out=kkn.ap, in_=kkn.ap), reads=kkn.keys, writes=kkn.keys)
                      stt(kkn.ap, k32.ap, P(17 + hp), kkn.ap, ALU.mult, ALU.mult, k32.keys + kkn.keys + PPk, kkn.keys)
                      kmod = lw
                      ts("dve", kmod.ap, a32.ap, P(20 + hp), omka[:, l, hp:hp + 1], ALU.mult, ALU.add, a32.keys + PPk, kmod.keys)
                      tt("dve", kmod.ap, kmod.ap, k32.ap, ALU.mult, kmod.keys + k32.keys, kmod.keys)
                      bvec = a32
                      tt("dve", bvec.ap, kkn.ap, a32.ap, ALU.mult, kkn.keys + a32.keys, bvec.keys)
                      cp("dve", pcs[:, hp, :], E.v("p (c j) -> p c j", j=64)[:, :, 63], E.keys, [("pcs", hp)])
                      arv = artT[hp].v("p (t q j) -> p t q j", t=4, q=2)
                      stt(arv[:, :, 0, :], kkn.v("p (t j) -> p t j", j=128), -1.0, Epv.v("p (t j) -> p t j", j=128),
                          ALU.mult, ALU.mult, kkn.keys + Epv.keys, artT[hp].keys)
                      tt("dve", arv[:, :, 1, :], r32.v("p (t j) -> p t j", j=128), E.v("p (t j) -> p t j", j=128), ALU.mult,
                         r32.keys + E.keys, artT[hp].keys)
                      tt("dve", btT[hp].ap, bvec.ap, Einv.ap, ALU.mult, bvec.keys + Einv.keys, btT[hp].keys)
                      tt("dve", ktT[hp].ap, kmod.ap, Einv.ap, ALU.mult, kmod.keys + Einv.keys, ktT[hp].keys)
                      tt("dve", bhb.ap, bvec.ap, Dd.ap, ALU.mult, bvec.keys + Dd.keys, bhb.keys)
                      tt("dve", khb.ap, kmod.ap, Dd.ap, ALU.mult, kmod.keys + Dd.keys, khb.keys)
                      cp("act", vbb.ap, v32.ap, v32.keys, vbb.keys)
                      stt(rkb.ap, r32.ap, P(23 + hp), kmod.ap, ALU.mult, ALU.mult, r32.keys + kmod.keys + PPk, rkb.keys)

                  def s3(hp):
                      lw, a32, E, Einv, Epv, Dd, r32, k32, v32, kkn = bufs(hp)
                      hs = slice(hp * 128, (hp + 1) * 128)
                      mm(B[3][:], bonesbf[:], rkb.ap, True, True, rkb.keys + ["bonesbf"], [Bk[3]])
                      tt("dve", bonus[hp].ap, B[3][:], v32.ap, ALU.mult, [Bk[3]] + v32.keys, bonus[hp].keys)
                      for t_ in range(NT):
                          Bb = B[4][:].bitcast(BF16)
                          for qi, src in enumerate((bhb, khb, vbb)):
                              tr(Bb[:, qi * 128:(qi + 1) * 128], src.ap[:, t_ * 128:(t_ + 1) * 128], identbf[:],
                                 src.keys + ["identbf"], [Bk[4]])
                          cp("act", tokv[:, t_, :, hs], Bb[:, 0:384].rearrange("p (q c) -> p q c", q=3), [Bk[4]], tokm.keys)

                  proj_stage(0)
                  s2a(0)
                  proj_stage(1)
                  s2a(1)
                  s2b(0)
                  s3(0)
                  s2b(1)
                  proj_stage(2)
                  s2a(2)
                  s3(1)
                  s2b(2)
                  s3(2)

                  mk.muted = mk.muted or CUT == "rw1"
                  yw = f[0:3]
                  Sk = f"S32_{l}"
                  for Sb_ in Sbf:
                      memset("dve", Sb_.ap, 0.0, Sb_.keys)
                  memset("dve", rhs_sb.ap, 0.0, rhs_sb.keys)
                  memset("dve", sa_sb.ap, 0.0, sa_sb.keys)

                  def put_state(dst):
                      dv = dst.ap[:, 0:384].rearrange("p (a h b) -> p a h b", a=3, h=2)
                      for hh in range(2):
                          po = hh * 64
                          cp("act", dv[po:po + 64, :, hh, :], S32[l][po:po + 64, :, :], [Sk], dst.keys)

                  put_state(Sbf[0])
                  def alias_bf(buf, n0, n):
                      s0 = buf.keys[0][1] + n0
                      return Buf(A[:, s0:s0 + n, :].rearrange("p a b -> p (a b)"), [("A", q_) for q_ in range(s0, s0 + n)])

                  LMs = [LM, [alias_bf(f[6], 0, 2), alias_bf(f[7], 0, 2), alias_bf(f[8], 0, 2)]]
                  LTs = [LT, [alias_bf(f[9], 0, 1), alias_bf(f[9], 1, 1), alias_bf(f2[0], 0, 1)]]
                  Nms = [Nm, [alias_bf(f2[0], 1, 1), alias_bf(f2[1], 0, 1), alias_bf(f2[1], 1, 1)]]

                  def mat(t_):
                      LM_, LT_, Nm_ = LMs[t_ % 2], LTs[t_ % 2], Nms[t_ % 2]
                      cs_ = slice(t_ * 128, (t_ + 1) * 128)
                      for hp in range(3):
                          arv = artT[hp].v("p (t q j) -> p t q j", t=4, q=2)
                          for hh in range(2):
                              po = hh * 64
                              ar_rhs = arv[po:po + 64, t_, :, :].rearrange("p q j -> p (q j)")
                              mm(B[hh][:, 0:256], btT[hp].ap[po:po + 64, cs_], ar_rhs, True, True,
                                 btT[hp].keys + artT[hp].keys, [Bk[hh]])
                              mm(B[hh][:, 256:512], ktT[hp].ap[po:po + 64, cs_], ar_rhs, True, True,
                                 ktT[hp].keys + artT[hp].keys, [Bk[hh]])
                              mm(B[2 + hh][:, 0:128], arv[po:po + 64, t_, 0, :], btT[hp].ap[po:po + 64, cs_],
                                 True, True, btT[hp].keys + artT[hp].keys, [Bk[2 + hh]])
                          for hh in range(2):
                              tt("dve", LM_[hp].ap[:, hh * 512:(hh + 1) * 512], B[hh][:], msk4[:], ALU.mult, [Bk[hh], "msk4"], LM_[hp].keys)
                              tt("dve", LT_[hp].ap[:, hh * 128:(hh + 1) * 128], B[2 + hh][:, 0:128], mskL[:, 0, :], ALU.mult,
                                 [Bk[2 + hh], "mskL"], LT_[hp].keys)
                          lmv = LM_[hp].v("p (h q j) -> p h q j", h=2, q=4)
                          nv = Nm_[hp].ap[:, 0:256].rearrange("p (h j) -> p h j", h=2)
                          tt("dve", nv, lmv[:, :, 0, :], identbf[:].rearrange("p (o j) -> p o j", o=1).to_broadcast([128, 2, 128]),
                             ALU.add, LM_[hp].keys + ["identbf"], Nm_[hp].keys)

                  def gen_dbl(t_):
                      LM_, LT_, Nm_ = LMs[t_ % 2], LTs[t_ % 2], Nms[t_ % 2]
                      Xc, XTc, Xk = [], [], []
                      for hp in range(3):
                          lmv = LM_[hp].v("p (h q j) -> p h q j", h=2, q=4)
                          Xc.append([lmv[:, 0, 0, :], lmv[:, 1, 0, :]])
                          XTc.append([LT_[hp].ap[:, 0:128], LT_[hp].ap[:, 128:256]])
                          Xk.append(LM_[hp].keys + LT_[hp].keys)
                      for lev in range(1, 6):
                          for hp in range(3):
                              bx = 4 + hp
                              bxv = B[bx][:].rearrange("p (h q j) -> p h q j", h=2, q=2)
                              for hh in range(2):
                                  if lev < 5:
                                      mm(bxv[:, hh, 0, :], XTc[hp][hh], Xc[hp][hh], True, True, Xk[hp], [Bk[bx]])
                                  mm(bxv[:, hh, 1, :], Xc[hp][hh], XTc[hp][hh], True, True, Xk[hp], [Bk[bx]])
                          yield
                          for hp in range(3):
                              bx = 4 + hp
                              xx = XX[hp][lev % 2]
                              xxv = xx.v("p (h q j) -> p h q j", h=2, q=2)
                              bxv = B[bx][:].rearrange("p (h q j) -> p h q j", h=2, q=2)
                              eng = "dve" if hp == 1 else "act"
                              if lev < 5:
                                  cp(eng, xx.ap, B[bx][:], [Bk[bx]], xx.keys)
                              else:
                                  cp(eng, xxv[:, :, 1, :], bxv[:, :, 1, :], [Bk[bx]], xx.keys)
                              Xc[hp] = [xxv[:, 0, 0, :], xxv[:, 1, 0, :]]
                              XTc[hp] = [xxv[:, 0, 1, :], xxv[:, 1, 1, :]]
                              Xk[hp] = xx.keys
                          yield
                          for hp in range(3):
                              bn = 4 + hp
                              nv = Nm_[hp].ap[:, 0:256].rearrange("p (h j) -> p h j", h=2)
                              for hh in range(2):
                                  mm(B[bn][:, hh * 128:(hh + 1) * 128], XTc[hp][hh], nv[:, hh, :], True, True,
                                     Xk[hp] + Nm_[hp].keys, [Bk[bn]])
                          yield
                          for hp in range(3):
                              bn = 4 + hp
                              nv = Nm_[hp].ap[:, 0:256].rearrange("p (h j) -> p h j", h=2)
                              tt("dve", nv, nv, B[bn][:, 0:256].rearrange("p (h j) -> p h j", h=2), ALU.add,
                                 Nm_[hp].keys + [Bk[bn]], Nm_[hp].keys)
                          yield

                  ccs = [0]

                  def gen_seq(t_):
                      LM_, LT_, Nm_ = LMs[t_ % 2], LTs[t_ % 2], Nms[t_ % 2]
                      for hf in range(2):
                          c = 2 * t_ + hf
                          tp = hf * 64
                          tsl = slice(tp, tp + 64)
                          Sc, Sn = Sbf[ccs[0] % 2], Sbf[(ccs[0] + 1) % 2]
                          ccs[0] += 1
                          sxv = Sc.ap[:, 0:384].rearrange("p (a h b) -> p a h b", a=3, h=2)
                          for h in range(6):
                              hp, hh = h // 2, h % 2
                              lmv = LM_[hp].v("p (h q j) -> p h q j", h=2, q=4)
                              arv = artT[hp].v("p (t q j) -> p t q j", t=4, q=2)
                              o_ = B[7][tsl, h * 64:(h + 1) * 64]
                              mm(o_, lmv[:, hh, 2, tsl], tokv[:, t_, 2, h * 64:(h + 1) * 64], True, False,
                                 LM_[hp].keys + tokm.keys, [Bk[7]])
                              mm(o_, arv[:, t_, 0, tsl], sxv[:, hp, hh, :], False, True,
                                 artT[hp].keys + Sc.keys, [Bk[7]])
                          yield
                          cp("act", rhs_sb.ap[tsl, 0:384], B[7][tsl, 0:384], [Bk[7]], rhs_sb.keys)
                          yield
                          for h in range(6):
                              hp, hh = h // 2, h % 2
                              nv = Nm_[hp].ap[:, 0:256].rearrange("p (h j) -> p h j", h=2)
                              mm(B[7][tsl, h * 64:(h + 1) * 64], nv[:, hh, tsl], rhs_sb.ap[:, h * 64:(h + 1) * 64], True, True,
                                 Nm_[hp].keys + rhs_sb.keys, [Bk[7]])
                          yield
                          cp("dve", sa_sb.ap[tsl, 0:384], B[7][tsl, 0:384], [Bk[7]], sa_sb.keys)
                          yield
                          bsn = 2 + hf
                          for h in range(6):
                              hp, hh = h // 2, h % 2
                              po = hh * 64
                              o_ = B[bsn][po:po + 64, hp * 64:(hp + 1) * 64]
                              mm(o_, tokv[tsl, t_, 1, h * 64:(h + 1) * 64], tokv[tsl, t_, 2, h * 64:(h + 1) * 64], True, False,
                                 tokm.keys, [Bk[bsn]])
                              mm(o_, tokv[tsl, t_, 0, h * 64:(h + 1) * 64], sa_sb.ap[tsl, h * 64:(h + 1) * 64], False, True,
                                 tokm.keys + sa_sb.keys, [Bk[bsn]])
                          yield
                          for h in range(6):
                              hp, hh = h // 2, h % 2
                              po = hh * 64
                              lmv = LM_[hp].v("p (h q j) -> p h q j", h=2, q=4)
                              arv = artT[hp].v("p (t q j) -> p t q j", t=4, q=2)
                              o_ = B[0][po:po + 64, hp * 64:(hp + 1) * 64]
                              mm(o_, sxv[:, hp, hh, :], arv[:, t_, 1, tsl], True, False, Sc.keys + artT[hp].keys, [Bk[0]])
                              mm(o_, sa_sb.ap[:, h * 64:(h + 1) * 64], lmv[:, hh, 1, tsl], False, False,
                                 sa_sb.keys + LM_[hp].keys, [Bk[0]])
                              mm(o_, tokv[:, t_, 2, h * 64:(h + 1) * 64], lmv[:, hh, 3, tsl], False, True,
                                 tokm.keys + LM_[hp].keys, [Bk[0]])
                          yield
                          for hp in range(3):
                              stt(S32[l][:, hp, :], S32[l][:, hp, :], pcs[:, hp, c:c + 1], B[bsn][:, hp * 64:(hp + 1) * 64],
                                  ALU.mult, ALU.add, [Sk, ("pcs", hp), Bk[bsn]], [Sk])
                          put_state(Sn)
                          yield
                          for hp in range(3):
                              cp("act", yw[hp].ap[:, c * 64:(c + 1) * 64], B[0][:, hp * 64:(hp + 1) * 64], [Bk[0]], yw[hp].keys)
                          yield

                  mat(0)
                  run_interleaved([gen_dbl(0)])
                  for t_ in range(NT):
                      if t_ + 1 < NT:
                          mat(t_ + 1)
                          run_interleaved([gen_seq(t_), gen_dbl(t_ + 1)])
                      else:
                          run_interleaved([gen_seq(t_)])
                  mk.muted = mk.muted or CUT == "rw3"
                  headnorm3(yw, 1, 26, 29, f[3:6], l)
                  for hp in range(3):
                      tt("dve", yw[hp].ap, yw[hp].ap, bonus[hp].ap, ALU.add, yw[hp].keys + bonus[hp].keys, yw[hp].keys)
                  for hp in range(3):
                      tt("dve", yT[:, hp, :], yw[hp].ap, gT[hp].ap, ALU.mult, yw[hp].keys + gT[hp].keys, [("yT", hp)])
                  if s == 0 and blk == 0 and l == 0:
                      tap("yT", yT[:].rearrange("p a b -> p (a b)"), YTK)

                mk.muted = False
                if "out" in STAGES:
                  load_ln(1 + l)
                  Wo = [wload(w_out_d[l][:, hf * 512:(hf + 1) * 512], 512) for hf in range(2)]
                  for t_ in range(NT):
                      sb_ = s32[t_ % 2]
                      sk = f"s32_{t_ % 2}"
                      for hf in range(2):
                          bk = 2 * (t_ % 2) + hf
                          for kc in range(8):
                              mm(B[bk][:], yT[:, kc, t_ * 128:(t_ + 1) * 128], Wo[hf][0][:, kc, :], kc == 0, kc == 7,
                                 [("yT", kc), Wo[hf][1]], [Bk[bk]])
                          stt(sb_[:, hf * 512:(hf + 1) * 512], xres[:, t_, hf * 512:(hf + 1) * 512], ALPHA, B[bk][:],
                              ALU.mult, ALU.add, [("xres", t_), Bk[bk]], [sk])
                      ln_a(t_, sb_[:], sk)
                      if t_ >= 1:
                          ln_t(t_ - 1, 4 + 2 * ((t_ - 1) % 2))
                  ln_t(NT - 1, 4 + 2 * ((NT - 1) % 2))

                if "ffn" in STAGES:
                  ar.reset()
                  hT = ar.bf(NJ)
                  u32 = [ar.f32(2) for _ in range(4)]
                  c32 = [ar.f32(1) for _ in range(4)]
                  hTv = hT.v("p (j t) -> p j t", j=NJ)
                  chk = f"chal{l}"
                  ui = 0
                  for jb in range(6):
                      ncol = 512 if jb < 5 else 256
                      Wg_, Wg_k = wload(w_up_d[l][:, jb * 512:jb * 512 + ncol], ncol)
                      Wv2, Wv2_k = wload(w_up_d[l][:, DFF + jb * 512:DFF + jb * 512 + ncol], ncol)
                      for jj in range(ncol // 128):
                          j = jb * 4 + jj
                          cres = []
                          for (Wx, Wx_k, cidx) in ((Wg_, Wg_k, j), (Wv2, Wv2_k, NJ + j)):
                              k2 = ui % 4
                              ui += 1
                              pbk = k2
                              for kc in range(8):
                                  mm(B[pbk][:], Wx[:, kc, jj * 128:(jj + 1) * 128], xT[:, kc, :], kc == 0, kc == 7,
                                     [Wx_k] + XTK, [Bk[pbk]])
                              u = u32[k2]
                              c_ = c32[k2]
                              cp("act", u.ap[:, 2:514], B[pbk][:], [Bk[pbk]], u.keys)
                              cp("dve", u.ap[:, 0:2], chal[l][:, cidx, :], [chk], u.keys)
                              act(c_.ap, B[pbk][:], AF.Identity, [Bk[pbk], "cw", "cb"], c_.keys, bias=cb[:, l, cidx:cidx + 1],
                                  scale=cw[:, l, cidx, 2:3])
                              stt(c_.ap, u.ap[:, 1:513], cw[:, l, cidx, 1:2], c_.ap, ALU.mult, ALU.add, u.keys + c_.keys + ["cw"], c_.keys)
                              stt(c_.ap, u.ap[:, 0:512], cw[:, l, cidx, 0:1], c_.ap, ALU.mult, ALU.add, u.keys + c_.keys + ["cw"], c_.keys)
                              cp("act", chal[l][:, cidx, :], u.ap[:, 512:514], u.keys, [chk])
                              cres.append(c_)
                          act(cres[0].ap, cres[0].ap, AF.Silu, cres[0].keys, cres[0].keys)
                          tt("dve", hTv[:, j, :], cres[0].ap, cres[1].ap, ALU.mult, cres[0].keys + cres[1].keys, [hT.keys[j]])
                  load_ln(1 + L + l)
                  first = True
                  for (j0, nj) in ((0, 8), (8, 8), (16, 6)):
                      for hf in range(2):
                          Wd, Wd_k = wload(w_dn_d[l][j0 * 128:(j0 + nj) * 128, hf * 512:(hf + 1) * 512], 512, nk=nj)
                          for jj in range(nj):
                              j = j0 + jj
                              for t_ in range(NT):
                                  bk = 2 * t_ + hf
                                  mm(B[bk][:], hTv[:, j, t_ * 128:(t_ + 1) * 128], Wd[:, jj, :], j == 0, j == NJ - 1,
                                     [hT.keys[j], Wd_k], [Bk[bk]])
                  for t_ in range(NT):
                      sb_ = s32[t_ % 2]
                      sk = f"s32_{t_ % 2}"
                      for hf in range(2):
                          bk = 2 * t_ + hf
                          stt(sb_[:, hf * 512:(hf + 1) * 512], xres[:, t_, hf * 512:(hf + 1) * 512], ALPHA, B[bk][:],
                              ALU.mult, ALU.add, [("xres", t_), Bk[bk]], [sk])
                      ln_a(t_, sb_[:], sk)
                      if t_ >= 1:
                          ln_t(t_ - 1, 2 * (t_ - 1))
                  ln_t(NT - 1, 2 * (NT - 1))
            oview = out_d[s, blk * TB:(blk + 1) * TB, :].rearrange("(t p) d -> p t d", p=128)
            for t_ in range(NT):
                mk.dma("sp", oview[:, t_, :], xres[:, t_, :], reads=[("xres", t_)], writes=[("out", s, blk, t_)])
    outkeys = [("out", s, blk, t_) for s in range(nseq) for blk in range(nblk) for t_ in range(NT)]
    mk.final_wait("sp", outkeys + ["tap_" + n for n in tap_d])
    mk.emit()
    mk.close()
    return nc, mk


def host_consts():
    f32 = np.float32
    p = np.arange(128)
    ident = np.eye(128, dtype=f32)
    bones = (p[:, None] // 64 == p[None, :] // 64).astype(f32)
    prot = np.zeros((128, 128), f32)
    for m in range(128):
        if m % 64 < 32:
            prot[m + 32, m] = -1.0
        else:
            prot[m - 32, m] = 1.0
    pos = np.arange(2048, dtype=f32)
    inv = (np.float32(10000.0) ** (-np.arange(32, dtype=f32) / np.float32(32))).astype(f32)
    ang = (pos[:, None] * inv[None, :]).astype(f32)
    cosv = np.cos(ang).astype(f32)
    sinv = np.sin(ang).astype(f32)
    jidx = (p % 64) % 32
    cosT = np.ascontiguousarray(cosv[:, jidx].T)
    sinT = np.ascontiguousarray(sinv[:, jidx].T)
    log_g = np.log(1.0 - np.exp2(-5.0 - np.arange(6, dtype=f32))).astype(f32)
    i64 = np.arange(64, dtype=f32)
    same = (p[:, None] // 64 == p[None, :] // 64)
    dist = np.abs(p[:, None] - p[None, :]).astype(f32)
    dmatT = np.zeros((128, 6, 128), f32)
    for h in range(6):
        dmatT[:, (h % 2) * 3 + h // 2, :] = np.where(same, np.exp(log_g[h] * dist), 0.0)
    kdq = np.zeros((128, 2, 3, 64), f32)
    cdt = np.zeros((128, 3), f32)
    for hp in range(3):
        for hh in range(2):
            h = 2 * hp + hh
            kdq[hh * 64:(hh + 1) * 64, 0, hp, :] = np.exp(log_g[h] * (63.0 - i64))[None, :]
            kdq[hh * 64:(hh + 1) * 64, 1, hp, :] = np.exp(log_g[h] * (i64 + 1.0))[None, :]
            cdt[hh * 64:(hh + 1) * 64, hp] = np.exp(log_g[h] * 64.0)
    su = (same & (p[:, None] < p[None, :])).astype(f32)
    uu = (same & (p[:, None] <= p[None, :])).astype(f32)
    sl = (same & (p[:, None] > p[None, :])).astype(f32)
    msk4 = np.concatenate([su, uu, su, uu], axis=1)
    cmask = np.ones((128, 512), f32)
    cmask[:, ::64] = 0.0
    return dict(ident=ident, bones=bones, prot=prot, cosT=cosT, sinT=sinT, dmatT=dmatT.reshape(128, 768),
                kdq=kdq.reshape(128, 384), cdt=cdt, msk4=msk4, mskL=sl, cmask=cmask)


def host_params(inp):
    f32 = np.float32
    g = lambda k: np.asarray(inp[k], dtype=f32)

    def col3(v):
        return np.ascontiguousarray(v.reshape(3, 128).T)

    pp = np.zeros((128, L, NPP), f32)
    for l in range(L):
        pp[:, l, 0:11] = g("rw_mu")[l].reshape(11, 128).T
        pp[:, l, 11:14] = col3(g("rw_w0")[l])
        pp[:, l, 14:17] = col3(g("rw_a0")[l])
        pp[:, l, 17:20] = col3(g("rw_k_k")[l])
        pp[:, l, 20:23] = col3(g("rw_k_a")[l])
        pp[:, l, 23:26] = col3(g("rw_r_k")[l].reshape(384))
        pp[:, l, 26:29] = col3(g("rw_ln_g")[l])
        pp[:, l, 29:32] = col3(g("rw_ln_b")[l])
        pp[:, l, 32:35] = col3(g("ret_gn_g")[l])
        pp[:, l, 35:38] = col3(g("ret_gn_b")[l])
    convw = np.ascontiguousarray(g("ffn_conv_w").transpose(0, 2, 1).reshape(L, 44, 128, 3).transpose(2, 0, 1, 3)).reshape(128, L, 132)
    convb = np.ascontiguousarray(g("ffn_conv_b").reshape(L, 44, 128).transpose(2, 0, 1))
    lorab = np.ascontiguousarray(np.concatenate([g("rw_w_up"), g("rw_a_up")], axis=1).transpose(1, 0, 2))
    gup = np.ascontiguousarray(g("rw_g_up").transpose(1, 0, 2))
    lng = np.concatenate([g("ln_in_g")[None], g("ln1_g"), g("ln2_g")], axis=0)
    lnb = np.concatenate([g("ln_in_b")[None], g("ln1_b"), g("ln2_b")], axis=0)
    rb = g("attn_rel_bias")
    ext = np.concatenate([rb, np.full((L, 4, 1), -30000.0, f32)], axis=-1)
    kj = np.arange(128)[:, None, None]
    dt = np.arange(5)[None, :, None]
    qi = np.arange(128)[None, None, :]
    rel = 128 * dt + qi - kj
    dch = 2 * dt + qi // 64 - kj // 64
    idx = np.clip(rel, -63, 128) + 63
    idx = np.where((dch >= 0) & (dch <= 8), idx, 192)
    biasT = ext[:, :, idx]
    biasT = np.ascontiguousarray(biasT.transpose(0, 2, 1, 3, 4)).reshape(L, 128, 4 * 5 * 128)
    return dict(w_in=g("w_in"), w_out=g("w_out"), w_up=g("ffn_w_up"), w_down=g("ffn_w_down"), lng=lng, lnb=lnb, pp=pp,
                convw=convw, convb=convb, lorab=lorab, gup=gup, biasT=biasT)


_CACHE = {}


def kernel(**inputs):
    x = np.asarray(inputs["x"], dtype=np.float32)
    Bt, S, _ = x.shape
    ncores = 8
    nseq = Bt // ncores
    nblk = S // TB
    key = (nseq, nblk)
    if key not in _CACHE:
        _CACHE[key] = build(nseq, nblk)[0]
    nc = _CACHE[key]
    shared = dict(host_consts())
    shared.update(host_params(inputs))
    in_maps = []
    for c in range(ncores):
        m = dict(shared)
        m["x"] = np.ascontiguousarray(x[c * nseq:(c + 1) * nseq])
        in_maps.append(m)
    res = run_bass_kernel_spmd(nc, in_maps, core_ids=list(range(ncores)))
    return np.concatenate([r["out"] for r in res.results], axis=0).astype(np.float32)
```
